# Optimizing a Trainium2 kernel written in Bass

```python
import math
import jax, jax.numpy as jnp
from jax import lax
import numpy as np

D_MODEL = 1024
BATCH = 8
SEQ = 2048
DEPTH = 1
DEC_BATCH = 16
DEC_SEQ = 16
PAST_LEN = 4096

CHUNK = 64
WINDOW = 128
WIN_CHUNKS = WINDOW // CHUNK
N_HEADS = 16
N_KV_HEADS = 4
HEAD_DIM = 64
GROUP_REP = N_HEADS // N_KV_HEADS
ATTN_WIDTH = N_HEADS * HEAD_DIM
KV_WIDTH = N_KV_HEADS * HEAD_DIM
POOL_WINDOWS = (2, 4, 8, 16)
N_POOL_GROUPS = len(POOL_WINDOWS)
POOL_WIDTH = D_MODEL
POOL_GROUP = POOL_WIDTH // N_POOL_GROUPS
POOL_HIST = max(POOL_WINDOWS) - 1
N_BUCKETS = 32
MAX_DISTANCE = 128
EPS = 1e-6
NEG_INF = -1e30
SPLIT_SIZES = (ATTN_WIDTH, KV_WIDTH, KV_WIDTH, ATTN_WIDTH, POOL_WIDTH, POOL_WIDTH, D_MODEL, D_MODEL)
IN_WIDTH = sum(SPLIT_SIZES)

kernel_name = "hybrid_swa_sink_pool_stream_step"


def rmsnorm(x, gain):
    xf = x.astype(jnp.float32)
    y = xf * lax.rsqrt(jnp.mean(xf * xf, axis=-1, keepdims=True) + EPS)
    return (y * gain.astype(jnp.float32)).astype(x.dtype)


def t5_bucket(rel):
    nb = N_BUCKETS // 2
    ret = jnp.where(rel > 0, nb, 0)
    n = jnp.abs(rel)
    max_exact = nb // 2
    large = max_exact + (jnp.log(jnp.maximum(n, 1).astype(jnp.float32) / max_exact)
                         / math.log(MAX_DISTANCE / max_exact) * (nb - max_exact)).astype(jnp.int32)
    large = jnp.minimum(large, nb - 1)
    return ret + jnp.where(n < max_exact, n, large)


def rel_position_bias(table, nq, nk):
    rel = jnp.arange(nk, dtype=jnp.int32)[None, :] - WINDOW - jnp.arange(nq, dtype=jnp.int32)[:, None]
    return jnp.transpose(table[t5_bucket(rel)], (2, 0, 1)).astype(jnp.float32)


def branch_inputs(x, norm_gain, w_in, q_gain, k_gain):
    h = rmsnorm(x, norm_gain)
    z = h @ w_in
    pts, acc = [], 0
    for s in SPLIT_SIZES[:-1]:
        acc += s
        pts.append(acc)
    q, k, v, ag, pu, pg, ma, mp = jnp.split(z, pts, axis=-1)
    B, N = x.shape[:2]
    q = rmsnorm(q.reshape(B, N, N_HEADS, HEAD_DIM), q_gain)
    k = rmsnorm(k.reshape(B, N, N_KV_HEADS, HEAD_DIM), k_gain)
    v = v.reshape(B, N, N_KV_HEADS, HEAD_DIM)
    return q, k, v, ag, pu, pg, ma, mp


def sink_attention(q, k, v, bias, sinks, mask):
    Q, K = q.shape[-3], k.shape[-3]
    qg = q.reshape(*q.shape[:-2], N_KV_HEADS, GROUP_REP, HEAD_DIM)
    logits = jnp.einsum('...qgrd,...kgd->...grqk', qg, k).astype(jnp.float32) * (HEAD_DIM ** -0.5)
    logits = logits + bias.reshape(N_KV_HEADS, GROUP_REP, Q, K)
    if mask is not None:
        logits = jnp.where(mask, logits, NEG_INF)
    s = sinks.astype(jnp.float32).reshape(N_KV_HEADS, GROUP_REP, 1, 1)
    m = jnp.maximum(jnp.max(logits, axis=-1, keepdims=True), s)
    p = jnp.exp(logits - m)
    denom = jnp.sum(p, axis=-1, keepdims=True) + jnp.exp(s - m)
    out = jnp.einsum('...grqk,...kgd->...qgrd', (p / denom).astype(v.dtype), v)
    return out.reshape(*out.shape[:-3], ATTN_WIDTH)


def prompt_window_attention(q, k, v, bias, sinks):
    B, S = q.shape[:2]
    nc = S // CHUNK

    def band(t):
        tp = jnp.pad(t, ((0, 0), (WINDOW, 0), (0, 0), (0, 0)))
        tb = tp.reshape(B, nc + WIN_CHUNKS, CHUNK, *t.shape[2:])
        return jnp.concatenate([tb[:, i:i + nc] for i in range(WIN_CHUNKS + 1)], axis=2)

    kw, vw = band(k), band(v)
    qb = q.reshape(B, nc, CHUNK, N_HEADS, HEAD_DIM)
    kpos = jnp.arange(nc)[:, None] * CHUNK - WINDOW + jnp.arange(CHUNK + WINDOW)[None, :]
    mask = (kpos >= 0)[None, :, None, None, None, :]
    out = sink_attention(qb, kw, vw, bias, sinks, mask)
    return out.reshape(B, S, ATTN_WIDTH)


def pool_branch(u, hist, start, pool_w, pool_scale):
    B, N, _ = u.shape
    full = jnp.concatenate([hist.astype(u.dtype), u], axis=1).astype(jnp.float32)
    cs0 = jnp.pad(jnp.cumsum(full, axis=1), ((0, 0), (1, 0), (0, 0)))
    pos = start + jnp.arange(N)
    means = []
    for g, w in enumerate(POOL_WINDOWS):
        sl = slice(g * POOL_GROUP, (g + 1) * POOL_GROUP)
        ssum = cs0[:, POOL_HIST + 1:POOL_HIST + 1 + N, sl] - cs0[:, POOL_HIST + 1 - w:POOL_HIST + 1 - w + N, sl]
        cnt = jnp.minimum(w, pos + 1).astype(jnp.float32)[None, :, None]
        means.append(ssum / cnt)
    mixed = (jnp.concatenate(means, axis=-1) - u.astype(jnp.float32)).astype(u.dtype)
    mixed = jnp.einsum('bngc,gcd->bngd', mixed.reshape(B, N, N_POOL_GROUPS, POOL_GROUP), pool_w)
    return mixed.reshape(B, N, POOL_WIDTH) * pool_scale


def merge(x, attn_o, ag, pool_o, pg, ma, mp, w_attn_br, w_pool_br, w_out):
    a = (attn_o * jax.nn.silu(ag)) @ w_attn_br
    p = (pool_o * jax.nn.silu(pg)) @ w_pool_br
    return x + (jax.nn.sigmoid(ma) * a + jax.nn.sigmoid(mp) * p) @ w_out


def setup_inputs(seed: int = 0) -> dict:
    key = jax.random.key(seed)
    ks = jax.random.split(key, 16)

    def nrm(k, shape, scale):
        return jax.random.normal(k, shape, jnp.float32) * scale

    return {
        "x_prompt": nrm(ks[0], (BATCH, SEQ, D_MODEL), 1.0),
        "x_sample": nrm(ks[1], (DEC_BATCH, DEC_SEQ, D_MODEL), 1.0),
        "state_attn_k": nrm(ks[2], (DEPTH, DEC_BATCH, WINDOW, N_KV_HEADS, HEAD_DIM), 1.0),
        "state_attn_v": nrm(ks[3], (DEPTH, DEC_BATCH, WINDOW, N_KV_HEADS, HEAD_DIM), 1.0),
        "state_pool": nrm(ks[4], (DEPTH, DEC_BATCH, POOL_HIST, POOL_WIDTH), 1.0),
        "norm_gain": 1.0 + nrm(ks[5], (DEPTH, D_MODEL), 0.1),
        "w_in": nrm(ks[6], (DEPTH, D_MODEL, IN_WIDTH), D_MODEL ** -0.5),
        "q_norm_gain": 1.0 + nrm(ks[7], (DEPTH, HEAD_DIM), 0.1),
        "k_norm_gain": 1.0 + nrm(ks[8], (DEPTH, HEAD_DIM), 0.1),
        "attn_sinks": nrm(ks[9], (DEPTH, N_HEADS), 1.0),
        "rel_bias": nrm(ks[10], (N_BUCKETS, N_HEADS), 0.5),
        "pool_w": nrm(ks[11], (DEPTH, N_POOL_GROUPS, POOL_GROUP, POOL_GROUP), POOL_GROUP ** -0.5),
        "pool_scale": 1.0 + nrm(ks[12], (DEPTH, POOL_WIDTH), 0.1),
        "w_attn_br": nrm(ks[13], (DEPTH, ATTN_WIDTH, D_MODEL), ATTN_WIDTH ** -0.5),
        "w_pool_br": nrm(ks[14], (DEPTH, POOL_WIDTH, D_MODEL), POOL_WIDTH ** -0.5),
        "w_out": nrm(ks[15], (DEPTH, D_MODEL, D_MODEL), D_MODEL ** -0.5),
    }


def reference(x_prompt, x_sample, state_attn_k, state_attn_v, state_pool, norm_gain, w_in,
              q_norm_gain, k_norm_gain, attn_sinks, rel_bias, pool_w, pool_scale,
              w_attn_br, w_pool_br, w_out):
    n_s = x_sample.shape[1]
    bias_p = rel_position_bias(rel_bias, CHUNK, CHUNK + WINDOW)
    bias_s = rel_position_bias(rel_bias, n_s, WINDOW + n_s)
    xp, xs = x_prompt, x_sample
    pk, pv, pp, sk, sv, sp = [], [], [], [], [], []
    for l in range(DEPTH):
        q, k, v, ag, pu, pg, ma, mp = branch_inputs(xp, norm_gain[l], w_in[l], q_norm_gain[l], k_norm_gain[l])
        attn_o = prompt_window_attention(q, k, v, bias_p, attn_sinks[l])
        hist0 = jnp.zeros((xp.shape[0], POOL_HIST, POOL_WIDTH), pu.dtype)
        pool_o = pool_branch(pu, hist0, 0, pool_w[l], pool_scale[l])
        pk.append(k[:, -WINDOW:])
        pv.append(v[:, -WINDOW:])
        pp.append(pu[:, -POOL_HIST:])
        xp = merge(xp, attn_o, ag, pool_o, pg, ma, mp, w_attn_br[l], w_pool_br[l], w_out[l])
        q, k, v, ag, pu, pg, ma, mp = branch_inputs(xs, norm_gain[l], w_in[l], q_norm_gain[l], k_norm_gain[l])
        k_all = jnp.concatenate([state_attn_k[l].astype(k.dtype), k], axis=1)
        v_all = jnp.concatenate([state_attn_v[l].astype(v.dtype), v], axis=1)
        attn_o = sink_attention(q, k_all, v_all, bias_s, attn_sinks[l], None)
        pool_o = pool_branch(pu, state_pool[l], PAST_LEN, pool_w[l], pool_scale[l])
        sk.append(k_all[:, -WINDOW:])
        sv.append(v_all[:, -WINDOW:])
        sp.append(jnp.concatenate([state_pool[l].astype(pu.dtype), pu], axis=1)[:, -POOL_HIST:])
        xs = merge(xs, attn_o, ag, pool_o, pg, ma, mp, w_attn_br[l], w_pool_br[l], w_out[l])
    return (xp, xs, jnp.stack(pk), jnp.stack(pv), jnp.stack(pp), jnp.stack(sk), jnp.stack(sv), jnp.stack(sp))
```

```python
from contextlib import ExitStack
import math
import numpy as np
import concourse.bass as bass
import concourse.mybir as mybir
from concourse.bass_utils import run_bass_kernel_spmd

F32 = mybir.dt.float32
BF16 = mybir.dt.bfloat16
ALU = mybir.AluOpType
AF = mybir.ActivationFunctionType

NCORES = 8
SEQ = 2048
D = 1024
NT = SEQ // 128
EPS = 1e-6
C_Q, C_K, C_V, C_AG, C_PU, C_PG, C_MA, C_MP = 0, 1024, 1280, 1536, 2560, 3584, 4608, 5632


class Ev:
    __slots__ = ("sem", "value")

    def __init__(self, sem, value):
        self.sem = sem
        self.value = value


class Res:
    __slots__ = ("name", "w", "r")

    def __init__(self, name):
        self.name = name
        self.w = None
        self.r = []


class Bank:
    def __init__(self, name, t):
        self.name = name
        self.t = t
        self.last = {}


class Stream:
    def __init__(self, name, sem):
        self.name = name
        self.sem = sem
        self.count = 0
        self.ops = []
        self.waited = {}


class Sched:
    def __init__(self, nc, es):
        self.nc = nc
        self.es = es
        self.streams = {}
        for n in ("pe", "act", "dve", "pool", "sp"):
            sem = es.enter_context(nc.semaphore("s_" + n))
            self.streams[n] = Stream(n, sem)
        self.dma_keys = {}
        self.out_events = []

    def _deps(self, reads, writes, extra):
        deps = [e for e in extra if e is not None]
        for r in reads:
            if r.w is not None:
                deps.append(r.w)
        for w in writes:
            if w.w is not None:
                deps.append(w.w)
            deps.extend(w.r)
        return deps

    def _waits(self, st, deps):
        best = {}
        for d in deps:
            k = id(d.sem)
            if k not in best or best[k].value < d.value:
                best[k] = d
        waits = []
        for k, d in best.items():
            if st.waited.get(k, -1) >= d.value:
                continue
            st.waited[k] = d.value
            waits.append(d)
        return waits

    def _commit(self, ev, reads, writes):
        for r in reads:
            r.r.append(ev)
        for w in writes:
            w.w = ev
            w.r = []

    def op(self, stream, fn, reads=(), writes=(), extra=(), signal=True, excl=()):
        st = self.streams[stream]
        deps = self._deps(reads, writes, extra)
        for b in excl:
            deps.extend(ev for s, ev in b.last.items() if s != stream)
        if stream == "pe":
            deps = [d for d in deps if d.sem is not st.sem]
        waits = self._waits(st, deps)
        if not signal:
            st.ops.append((waits, fn, None))
            return None
        st.count += 1
        ev = Ev(st.sem, st.count)
        st.ops.append((waits, fn, (st.sem, 1)))
        self._commit(ev, reads, writes)
        for b in excl:
            b.last[stream] = ev
        return ev

    def dma(self, stream, fn, key, reads=(), writes=(), extra=(), is_output=False):
        st = self.streams[stream]
        if key not in self.dma_keys:
            sem = self.es.enter_context(self.nc.semaphore("d_" + key))
            self.dma_keys[key] = [sem, 0]
        ent = self.dma_keys[key]
        deps = [d for d in self._deps(reads, writes, extra) if d.sem is not ent[0]]
        waits = self._waits(st, deps)
        ent[1] += 16
        ev = Ev(ent[0], ent[1])
        st.ops.append((waits, fn, (ent[0], 16)))
        self._commit(ev, reads, writes)
        if is_output:
            self.out_events.append(ev)
        return ev

    def group_total(self, key):
        ent = self.dma_keys[key]
        return Ev(ent[0], ent[1])

    def emit(self, block):
        sch = self

        def run(st, eng):
            for waits, fn, inc in st.ops:
                for w in waits:
                    eng.wait_ge(w.sem, w.value)
                ins = fn(eng)
                if inc is not None:
                    ins.then_inc(inc[0], inc[1])

        finals = [Ev(ent[0], ent[1]) for ent in self.dma_keys.values()]

        @block.sync
        def _(e):
            run(sch.streams["sp"], e)
            for ev in finals:
                e.wait_ge(ev.sem, ev.value)

        @block.scalar
        def _(e):
            run(sch.streams["act"], e)

        @block.vector
        def _(e):
            run(sch.streams["dve"], e)

        @block.gpsimd
        def _(e):
            run(sch.streams["pool"], e)

        @block.tensor
        def _(e):
            run(sch.streams["pe"], e)


def L(f, *args, **kw):
    return lambda e: getattr(e, f)(*args, **kw)


def build_program(dbg=None):
    dbg = dbg or {}
    nc = bass.Bass("TRN2", target_bir_lowering=False, dynamic_dma_scratch_size=2048)

    def din(name, shape):
        return nc.dram_tensor(name, shape, F32, kind="ExternalInput").ap()

    def dout(name, shape):
        return nc.dram_tensor(name, shape, F32, kind="ExternalOutput").ap()

    xp = din("xp", [SEQ, D])
    xs = din("xs", [32, D])
    sk = din("sk", [2, 128, 256])
    sv = din("sv", [2, 128, 256])
    spl = din("spl", [2, 15, D])
    ng = din("ng", [1, D])
    win = din("win", [D, 6656])
    qg = din("qg", [1, 64])
    kg = din("kg", [1, 64])
    sinks = din("sinks", [1, 16])
    rb = din("rb", [32, 16])
    pw = din("pw", [4, 256, 256])
    psc = din("psc", [1, D])
    wa = din("wa", [D, D])
    wp = din("wp", [D, D])
    wo = din("wo", [D, D])
    cG = din("cG", [32, 384])
    cI = din("cI", [128, 128])
    cBD = din("cBD", [128, 128])
    cINV = din("cINV", [128, 8 * 16])

    yp = dout("yp", [SEQ, D])
    ys = dout("ys", [32, D])
    pk = dout("pk", [128, 256])
    pv = dout("pv", [128, 256])
    pp = dout("pp", [15, D])
    sko = dout("sko", [2, 128, 256])
    svo = dout("svo", [2, 128, 256])
    spo = dout("spo", [2, 15, D])
    scr = nc.dram_tensor("scr", [16, 128 * 384], F32, kind="Internal").ap()

    with ExitStack() as es:
        def sb(name, shape, dt):
            return es.enter_context(nc.sbuf_tensor(name, shape, dt))

        def ps(name, shape, dt):
            return es.enter_context(nc.psum_tensor(name, shape, dt))

        win_sb = sb("win_sb", [128, 8, 6656], BF16)
        wa_sb = sb("wa_sb", [128, 8, D], BF16)
        wp_sb = sb("wp_sb", [128, 8, D], BF16)
        wo_sb = sb("wo_sb", [128, 8, D], BF16)
        pw_sb = sb("pw_sb", [128, 4, 2, 256], BF16)
        xbuf = sb("xbuf", [128, 2, D], F32)
        hp = sb("hp", [128, D], BF16)
        hT = sb("hT", [128, 8, 128], BF16)
        gain_bc = sb("gain_bc", [128, D], BF16)
        E = sb("E", [128, 2, 16, 128], BF16)
        qT = sb("qT", [128, 8, 128], BF16)
        kTr = sb("kTr", [128, 2, 2, 128], BF16)
        vr = sb("vr", [128, 2, 256], BF16)
        ptraw = sb("ptraw", [128, 2, 512], BF16)
        pt = sb("pt", [128, 4, 512], BF16)
        dsb = sb("dsb", [128, 512], F32)
        o1 = sb("o1", [128, 512], F32)
        g_ag = sb("g_ag", [128, 8, 128], BF16)
        g_pg = sb("g_pg", [128, 8, 128], BF16)
        a_in = sb("a_in", [128, 8, 128], BF16)
        p_in = sb("p_in", [128, 8, 128], BF16)
        mT = sb("mT", [128, 8, 128], BF16)
        th_ma = sb("th_ma", [128, 8, 128], BF16)
        th_mp = sb("th_mp", [128, 8, 128], BF16)
        pu_sb = sb("pu_sb", [128, 8, 144], F32)
        sA = sb("sA", [128, 2, 144], F32)
        sB = sb("sB", [128, 2, 144], F32)
        mixed = sb("mixed", [128, 8, 128], BF16)
        t1 = sb("t1", [128, 2, 128], F32)
        t2 = sb("t2", [128, 2, 128], F32)
        pus = sb("pus", [128, 8, 2, 32], F32)
        ident_bf = sb("ident_bf", [128, 128], BF16)
        identf = sb("identf", [128, 128], F32)
        bd_bf = sb("bd_bf", [128, 128], BF16)
        ones_bf = sb("ones_bf", [128, 64], BF16)
        qgcol = sb("qgcol", [128, 1], F32)
        kgcol = sb("kgcol", [128, 1], F32)
        sink_t = sb("sink_t", [128, 8], F32)
        esink = sb("esink", [128, 8], F32)
        psc8 = sb("psc8", [128, 8], F32)
        st = sb("st", [128, 2, 4], F32)
        kf32 = o1[:, 0:256].rearrange("p (a b) -> p a b", a=2)
        kout = o1[:, 256:512]
        sq = ptraw
        rs = dsb
        G_sb = o1[0:32, 0:384]
        pu_bf = pu_sb[:, :, :].rearrange("p a b -> p (a b)").bitcast(BF16)
        kTst = pu_bf[:, 0:512].rearrange("p (a b c) -> p a b c", a=2, b=2)
        vst = pu_bf[:, 512:1024].rearrange("p (a c) -> p a c", a=2)
        stk_tm = mixed[:, 0:4, :].rearrange("p (a b) c -> p a (b c)", a=2)
        invc = mT[:, 0:2, :].bitcast(F32).rearrange("p a (b c) -> p (a b) c", c=16)
        tb = o1[0:32, 384:400]
        S_sb = dsb[0:16, 0:384]

        psT = ps("psT", [128, 8, 128], BF16)
        pbt = [ps("pb%d" % i, [128, 512], F32) for i in range(7)]
        BT = Bank("psT", psT)
        BB = [Bank("bb%d" % i, pbt[i]) for i in range(3)]
        BS = [Bank("bs%d" % i, pbt[3 + i]) for i in range(2)]
        BO = Bank("bo", pbt[5])
        BD = Bank("bd", pbt[6])
        ps_o, ps_d = BO.t, BD.t

        sc = Sched(nc, es)
        R = {}

        def res(n):
            if n not in R:
                R[n] = Res(n)
            return R[n]

        block = es.enter_context(nc.Block())

        def cdma(out, in_, wr, slow=False):
            if slow:
                return sc.dma("sp", L("dma_start", out=out, in_=in_, allow_slow_non_contiguous=True), "c", writes=[res(wr)])
            return sc.dma("sp", L("dma_start", out=out, in_=in_), "c", writes=[res(wr)])

        _t0 = dbg.get("tiles", [0])
        if _t0:
            _n0 = 128 if _t0[0] < NT else 32
            _src0 = xp[_t0[0] * 128:(_t0[0] + 1) * 128, :] if _t0[0] < NT else xs
            sc.dma("sp", L("dma_start", out=xbuf[0:_n0, _t0[0] % 2, :], in_=_src0), "x%d" % (_t0[0] % 2), writes=[res("xb%d" % (_t0[0] % 2))])
        cdma(tb, rb, "o1")
        cdma(G_sb, cG, "o1")
        cdma(identf[:, :], cI, "identf")
        for hf in range(2):
            cdma(qgcol[hf * 64:(hf + 1) * 64, :], qg.rearrange("o d -> d o"), "qgcol%d" % hf)
            cdma(kgcol[hf * 64:(hf + 1) * 64, :], kg.rearrange("o d -> d o"), "kgcol%d" % hf)
            cdma(sink_t[hf * 64:(hf + 1) * 64, :].rearrange("p (a b) -> p a b", a=2),
                 bass.AP(tensor=sinks.tensor, offset=4 * hf, ap=[[0, 64], [8, 2], [1, 4]]), "sink%d" % hf)
        cdma(psc8[:, :], psc.rearrange("o (b p) -> p (o b)", p=128), "psc8", slow=True)
        cdma(invc, cINV.rearrange("p (a b) -> p a b", a=8), "mT")
        R["qgcol"] = res("qgcol1")
        R["kgcol"] = res("kgcol1")
        c_all = None

        sc.op("pool", L("memset", ones_bf[:, :], 1.0), writes=[res("ones")])
        sc.op("pool", L("memset", pu_sb[:, :, 0:16], 0.0), writes=[res("pu")])
        sc.op("pool", L("memset", st[:, :, :], 0.0), writes=[res("st0"), res("st1")])
        def wdma(out, in_, key):
            sc.dma("pool", L("dma_start", out=out, in_=in_), key)

        wdma(ident_bf[:, :], cI, "c2")
        wdma(bd_bf[:, :], cBD, "c2")
        wdma(gain_bc[:, :], bass.AP(tensor=ng.tensor, offset=0, ap=[[0, 128], [1, D]]), "c2")

        def w_nat(wsb, wsrc, c0, C, key):
            wdma(wsb[:, :, c0:c0 + C], wsrc[:, c0:c0 + C].rearrange("(kc p) c -> p kc c", p=128), key)

        w_nat(win_sb, win, C_K, 512, "w_kv")
        for i in range(2):
            w_nat(win_sb, win, C_PU + 512 * i, 512, "w_pu")
        for i in range(2):
            w_nat(win_sb, win, C_PG + 512 * i, 512, "w_pg")
        for i in range(2):
            w_nat(win_sb, win, C_MP + 512 * i, 512, "w_mp")
        wdma(pw_sb[:, :, :, :], pw.rearrange("g (kc p) c -> p g kc c", p=128), "w_pw")
        for kb in range(2):
            for hf in range(2):
                wdma(wa_sb[hf * 64:(hf + 1) * 64, kb * 4:(kb + 1) * 4, :],
                     wa[512 * kb + 256 * hf:512 * kb + 256 * hf + 256, :].rearrange("(r d) c -> d r c", d=64), "w_wa")
        for i in range(2):
            w_nat(wp_sb, wp, 512 * i, 512, "w_wp")
        for i in range(2):
            w_nat(wo_sb, wo, 512 * i, 512, "w_wo")
        WEV = {k: sc.group_total(k) for k in ["c2", "w_kv", "w_pu", "w_pg", "w_pw", "w_mp", "w_wa", "w_wp", "w_wo"]}
        WEV["w_v"] = WEV["w_kv"]
        c_all = sc.group_total("c")

        sc.op("act", L("activation", out=esink[:, :], in_=sink_t[:, :], func=AF.Exp), extra=[c_all], writes=[res("esink")])

        sc.op("pe", L("matmul", BB[0].t[0:16, 0:384], lhsT=tb, rhs=G_sb, start=True, stop=True), extra=[c_all], reads=[res("o1")], excl=[BB[0]])
        sc.op("act", L("activation", out=S_sb, in_=BB[0].t[0:16, 0:384], func=AF.Copy), writes=[res("dsb")], excl=[BB[0]])
        sc.dma("sp", L("dma_start", out=scr.rearrange("h (r i) -> h r i", i=384), in_=S_sb.unsqueeze(1).to_broadcast([16, 128, 384])), "scrw", reads=[res("dsb")], writes=[res("scr")])
        def bias_finish():
            for ab, c0 in ((0, 255), (1, 127)):
                for hh in range(2):
                    sc.dma("sp", L("dma_start", out=xbuf[:, 1, :].rearrange("p (h q) -> p h q", h=8),
                                   in_=bass.AP(tensor=scr.tensor, offset=c0 + hh * 8 * 128 * 384, ap=[[383, 128], [128 * 384, 8], [1, 128]])),
                           "toe", reads=[res("scr")], writes=[res("xb1")])
                    sc.op("act", L("activation", out=E[:, ab, hh * 8:(hh + 1) * 8, :].rearrange("p h q -> p (h q)"), in_=xbuf[:, 1, :], func=AF.Exp), reads=[res("xb1")], writes=[res("E")])
            sc.op("dve", L("memset", E[0:64, 0, :, 64:128], 0.0), writes=[res("E")])
            sc.op("dve", L("memset", E[64:128, 1, :, 0:64], 0.0), writes=[res("E")])


        stage_bufs = [(th_ma, "th_ma"), (th_mp, "th_mp"), (p_in, "p_in"), (g_pg, "g_pg")]
        stage_ctr = [0]
        WQ = {}

        def load_perm(c0, name, perm=True):
            rounds = [(kb, kc) for kb in range(2) for kc in range(8)]
            slots = []

            def issue(i):
                kb, kc = rounds[i]
                buf, rn = stage_bufs[stage_ctr[0] % 4]
                stage_ctr[0] += 1
                stg = buf[:, :, :].rearrange("p a b -> p (a b)").bitcast(F32)
                sc.dma("act", L("dma_start", out=stg, in_=win[kc * 128:(kc + 1) * 128, c0 + kb * 512:c0 + (kb + 1) * 512]), "wst_" + rn, writes=[res(rn)])
                slots.append((stg, rn))

            for i in range(4):
                issue(i)
            ev = None
            for i, (kb, kc) in enumerate(rounds):
                stg, rn = slots[i]
                if perm:
                    ev = sc.op("act", L("activation", out=win_sb[:, kc, c0 + kb * 512:c0 + (kb + 1) * 512].rearrange("p (r hf d) -> p r hf d", r=4, hf=2),
                                        in_=stg.rearrange("p (hf r d) -> p r hf d", hf=2, r=4), func=AF.Copy), reads=[res(rn)])
                else:
                    ev = sc.op("act", L("activation", out=win_sb[:, kc, c0 + kb * 512:c0 + (kb + 1) * 512], in_=stg, func=AF.Copy), reads=[res(rn)])
                if i + 4 < len(rounds):
                    issue(i + 4)
            WQ[name] = ev


        bb_ctr = [0]

        ROT = BB + BS

        held = set()

        def next_bank():
            while True:
                b = ROT[bb_ctr[0] % len(ROT)]
                bb_ctr[0] += 1
                if b.name not in held:
                    return b

        pt_ctr = [0]
        sq_ctr = [0]
        s_ctr = [0]
        t_ctr = [0]

        def V(ap2d, nb, n):
            return ap2d[:, 0:nb * 128].rearrange("p (b c) -> p b c", c=128)[:, :, 0:n]

        xloaded = set(dbg.get("tiles", [0])[:1])

        def xload(t):
            sl = t % 2
            n = 128 if t < NT else 32
            src_ = xp[t * 128:(t + 1) * 128, :] if t < NT else xs
            sc.dma("sp", L("dma_start", out=xbuf[0:n, sl, :], in_=src_), "x%d" % sl, writes=[res("xb%d" % sl)])
            xloaded.add(t)

        def xprep_a(t):
            sl = t % 2
            n = 128 if t < NT else 32
            src_ = xp[t * 128:(t + 1) * 128, :] if t < NT else xs
            if not (dbg.get("noprefetch") is None and t in xloaded):
                sc.dma("sp", L("dma_start", out=xbuf[0:n, sl, :], in_=src_), "x%d" % sl, writes=[res("xb%d" % sl)])
            sc.op("act", L("activation", out=hp[0:n, :], in_=xbuf[0:n, sl, :], func=AF.Square, scale=1.0 / 32.0, accum_out=st[0:n, sl, 0:1]),
                  reads=[res("xb%d" % sl)], writes=[res("hp"), res("st%d" % sl)])
            sc.op("act", L("activation", out=st[0:n, sl, 1:2], in_=st[0:n, sl, 0:1], func=AF.Ln, bias=EPS), reads=[res("st%d" % sl)], writes=[res("st%d" % sl)])
            sc.op("act", L("activation", out=st[0:n, sl, 2:3], in_=st[0:n, sl, 1:2], func=AF.Exp, scale=-0.5), reads=[res("st%d" % sl)], writes=[res("st%d" % sl)])
            sc.op("dve", L("scalar_tensor_tensor", out=hp[0:n, :], in0=xbuf[0:n, sl, :], scalar=st[0:n, sl, 2:3], in1=gain_bc[0:n, :], op0=ALU.mult, op1=ALU.mult),
                  reads=[res("xb%d" % sl), res("st%d" % sl)], writes=[res("hp")], extra=[WEV["c2"]])
            sc.op("pool", L("memset", st[0:n, sl, 0:1], 0.0), reads=[], writes=[res("st%d" % sl)])

        def xprep_b(t):
            n = 128 if t < NT else 32
            for kc in range(8):
                sc.op("pe", L("transpose", out=psT[:, kc, 0:n], in_=hp[0:n, kc * 128:(kc + 1) * 128], identity=ident_bf[0:n, 0:n]),
                      reads=[res("hp")], excl=[BT], extra=[WEV["c2"]], signal=(kc == 7))
            sc.op("act", L("activation", out=hT[:, :, 0:n], in_=psT[:, :, 0:n], func=AF.Copy), writes=[res("hT")], excl=[BT])

        def mm_group(n, cols, wev):
            b = next_bank()
            nb = len(cols)
            for bi, col0 in enumerate(cols):
                for kc in range(8):
                    sc.op("pe", L("matmul", b.t[:, bi * 128:bi * 128 + n], lhsT=win_sb[:, kc, col0:col0 + 128], rhs=hT[:, kc, 0:n], start=(kc == 0), stop=(kc == 7)),
                          reads=[res("hT")], excl=[b], extra=[wev], signal=(bi == nb - 1 and kc == 7))
            return b

        def qk_a(t, typ, idxs):
            n = 128 if t < NT else 32
            nb = len(idxs)
            base = C_K if typ == "k" else C_Q
            b = mm_group(n, [base + 128 * i for i in idxs], WEV["w_kv"] if typ == "k" else WQ["q"])
            sqs = sq_ctr[0] % 2
            sq_ctr[0] += 1
            sc.op("act", L("activation", out=V(sq[:, sqs, :], nb, n), in_=V(b.t, nb, n), func=AF.Square), writes=[res("ptraw%d" % sqs)], excl=[b])
            held.add(b.name)
            return (b, sqs)

        def qk_b(t, typ, idxs, st_):
            b, sqs = st_
            n = 128 if t < NT else 32
            par = t % 2
            last = (t >= NT - 1)
            nb = len(idxs)
            b2 = next_bank()
            sqv = sq[:, sqs, :]
            if n == 128:
                sc.op("pe", L("matmul", V(b2.t, nb, n), lhsT=bd_bf[:, :], rhs=V(sqv, nb, n), start=True, stop=True), reads=[res("ptraw%d" % sqs)], excl=[b2], extra=[WEV["c2"]])
            else:
                for bi_ in range(nb):
                    sc.op("pe", L("matmul", b2.t[:, bi_ * 128:bi_ * 128 + n], lhsT=bd_bf[:, :], rhs=sqv[:, bi_ * 128:bi_ * 128 + n], start=True, stop=True),
                          reads=[res("ptraw%d" % sqs)], excl=[b2], extra=[WEV["c2"]], signal=(bi_ == nb - 1))
            sc.op("act", L("activation", out=V(rs, nb, n), in_=V(b2.t, nb, n), func=AF.Ln, bias=EPS), writes=[res("dsb")], excl=[b2])
            sc.op("act", L("activation", out=V(rs, nb, n), in_=V(rs, nb, n), func=AF.Exp, scale=-0.5), reads=[res("dsb")], writes=[res("dsb")])
            if typ == "k":
                dst, dres, gcol = kTr[:, :, par, 0:n], res("kT%d" % par), kgcol
            else:
                dst, dres, gcol = qT[:, idxs[0]:idxs[0] + nb, 0:n], res("qT"), qgcol
            sc.op("dve", L("scalar_tensor_tensor", out=dst, in0=V(b.t, nb, n), scalar=gcol[:, 0:1], in1=V(rs, nb, n), op0=ALU.mult, op1=ALU.mult),
                  reads=[res("dsb")], writes=[dres], excl=[b], extra=[c_all])
            if typ == "k" and last:
                sc.op("dve", L("scalar_tensor_tensor", out=kf32[:, :, 0:n], in0=V(b.t, nb, n), scalar=gcol[:, 0:1], in1=V(rs, nb, n), op0=ALU.mult, op1=ALU.mult),
                      reads=[res("dsb")], writes=[res("o1")], excl=[b])
            held.discard(b.name)

        def v_group(t):
            n = 128 if t < NT else 32
            par = t % 2
            last = (t >= NT - 1)
            if t < NT:
                b = next_bank()
                for kc in range(8):
                    sc.op("pe", L("matmul", b.t[0:n, 0:256], lhsT=hT[:, kc, 0:n], rhs=win_sb[:, kc, C_V:C_V + 256], start=(kc == 0), stop=(kc == 7)),
                          reads=[res("hT")], excl=[b], extra=[WEV["w_v"]], signal=(kc == 7))
                sc.op("act", L("activation", out=vr[0:n, par, :], in_=b.t[0:n, 0:256], func=AF.Copy), writes=[res("vr%d" % par)], excl=[b])
                if last:
                    vo = t1[:, :, :].rearrange("p a b -> p (a b)")
                    sc.op("act", L("activation", out=vo[0:n, :], in_=b.t[0:n, 0:256], func=AF.Copy), writes=[res("t1_0"), res("t1_1")], excl=[b])
                    sc.dma("sp", L("dma_start", out=pv, in_=vo[0:n, :]), "o_v0", reads=[res("t1_0"), res("t1_1")], is_output=True)
            else:
                for bq in range(2):
                    b = next_bank()
                    for kc in range(8):
                        sc.op("pe", L("matmul", b.t[0:16, 0:256], lhsT=hT[:, kc, bq * 16:(bq + 1) * 16], rhs=win_sb[:, kc, C_V:C_V + 256], start=(kc == 0), stop=(kc == 7)),
                              reads=[res("hT")], excl=[b], extra=[WEV["w_v"]], signal=(kc == 7))
                    sc.op("act", L("activation", out=vr[0:16, bq, :], in_=b.t[0:16, 0:256], func=AF.Copy), writes=[res("vr%d" % bq)], excl=[b])
                    tt = t1 if bq == 0 else t2
                    vo = tt[:, :, :].rearrange("p a b -> p (a b)")
                    tr = [res("t%d_0" % (bq + 1)), res("t%d_1" % (bq + 1))]
                    sc.op("act", L("activation", out=vo[0:16, :], in_=b.t[0:16, 0:256], func=AF.Copy), writes=tr, excl=[b])
                    sc.dma("sp", L("dma_start", out=svo[bq, 112:128, :], in_=vo[0:16, :]), "o_v%d" % bq, reads=tr, is_output=True)

        def k_out(t):
            n = 128 if t < NT else 32
            b = next_bank()
            for kb in range(2):
                sc.op("pe", L("transpose", out=b.t[0:n, kb * 128:(kb + 1) * 128], in_=kf32[:, kb, 0:n], identity=identf[:, :]),
                      reads=[res("o1")], excl=[b], extra=[c_all], signal=(kb == 1))
            sc.op("act", L("activation", out=kout[0:n, :], in_=b.t[0:n, 0:256], func=AF.Copy), writes=[res("o1")], excl=[b])
            if t < NT:
                sc.dma("sp", L("dma_start", out=pk, in_=kout[0:n, :]), "o_k", reads=[res("o1")], is_output=True)
            else:
                for bq in range(2):
                    sc.dma("sp", L("dma_start", out=sko[bq, 112:128, :], in_=kout[bq * 16:(bq + 1) * 16, :]), "o_k", reads=[res("o1")], is_output=True)

        def stage_blocks(t, typ, only=None):
            n = 128 if t < NT else 32
            for gi in (range(2) if only is None else [only]):
                idxs = [4 * gi + i for i in range(4)]
                if typ == "pu":
                    b = mm_group(n, [C_PU + 128 * i for i in idxs], WEV["w_pu"])
                    if t < NT:
                        sc.op("act", L("activation", out=pu_sb[:, 4 * gi:4 * gi + 4, 16:16 + n], in_=V(b.t, 4, n), func=AF.Copy), writes=[res("pu")], excl=[b])
                    else:
                        for i in range(4):
                            sc.op("act", L("activation", out=pus[:, 4 * gi + i, :, 16:32], in_=b.t[:, i * 128:i * 128 + 32].rearrange("p (b c) -> p b c", b=2), func=AF.Copy),
                                  writes=[res("pus")], excl=[b], extra=[c_all])
                elif typ in ("ag", "pg"):
                    c0, wev_, dst = (C_AG, WQ["ag"], g_ag) if typ == "ag" else (C_PG, WEV["w_pg"], g_pg)
                    b = mm_group(n, [c0 + 128 * i for i in idxs], wev_)
                    sc.op("act", L("activation", out=dst[:, 4 * gi:4 * gi + 4, 0:n], in_=V(b.t, 4, n), func=AF.Silu), writes=[res("g_" + typ)], excl=[b])
                else:
                    c0, wev_, dst = (C_MA, WQ["ma"], th_ma) if typ == "ma" else (C_MP, WEV["w_mp"], th_mp)
                    b = mm_group(n, [c0 + 128 * i for i in idxs], wev_)
                    sc.op("act", L("activation", out=dst[:, 4 * gi:4 * gi + 4, 0:n], in_=V(b.t, 4, n), func=AF.Tanh, scale=0.5), writes=[res("th_" + typ)], excl=[b])
                yield

        def attention(t):
            par = t % 2
            if t < NT:
                seqs = [(0, 128)]
            else:
                seqs = [(0, 16), (16, 16)]
            for bi, (q0, nq) in enumerate(seqs):
                if t < NT:
                    ktiles = []
                    if t > 0:
                        ktiles.append((lambda hf, kb: kTr[hf * 64:(hf + 1) * 64, kb, 1 - par, 0:128], lambda g: vr[0:128, 1 - par, g * 64:(g + 1) * 64], 128, 0,
                                       [res("kT%d" % (1 - par)), res("vr%d" % (1 - par))]))
                    ktiles.append((lambda hf, kb: kTr[hf * 64:(hf + 1) * 64, kb, par, 0:128], lambda g: vr[0:128, par, g * 64:(g + 1) * 64], 128, 1,
                                   [res("kT%d" % par), res("vr%d" % par)]))
                else:
                    ktiles = [(lambda hf, kb, bi=bi: kTst[hf * 64:(hf + 1) * 64, bi, kb, :], lambda g, bi=bi: vst[0:128, bi, g * 64:(g + 1) * 64], 128, 0, [res("pu")]),
                              (lambda hf, kb, q0=q0: kTr[hf * 64:(hf + 1) * 64, kb, par, q0:q0 + 16], lambda g, bi=bi: vr[0:16, bi, g * 64:(g + 1) * 64], 16, 1,
                               [res("kT%d" % par), res("vr%d" % bi)])]
                W = 4 * nq

                def scores_kb(kb):
                    ptl = {0: [], 1: []}
                    for (kget, vget, nk, ab, kres) in ktiles:
                        banks = [next_bank(), next_bank()]
                        for hf in range(2):
                            sc.op("pe", L("matmul", banks[hf].t[0:nk, 0:W].rearrange("p (r q) -> p r q", r=4),
                                          lhsT=kget(hf, kb), rhs=qT[hf * 64:(hf + 1) * 64, kb * 4:(kb + 1) * 4, q0:q0 + nq], start=True, stop=True),
                                  reads=[res("qT"), kres[0]], excl=[banks[hf]])
                        for hf in range(2):
                            sc.op("act", L("activation", out=ptraw[0:nk, hf, 0:W], in_=banks[hf].t[0:nk, 0:W], func=AF.Exp, scale=0.125),
                                  writes=[res("ptraw%d" % hf)], excl=[banks[hf]])
                        for hf in range(2):
                            g = 2 * kb + hf
                            pi = pt_ctr[0] % 4
                            pt_ctr[0] += 1
                            sc.op("dve", L("tensor_tensor", out=pt[0:nk, pi, 0:W].rearrange("p (r q) -> p r q", r=4),
                                           in0=ptraw[0:nk, hf, 0:W].rearrange("p (r q) -> p r q", r=4),
                                           in1=E[0:nk, ab, g * 4:(g + 1) * 4, 0:nq], op=ALU.mult),
                                  reads=[res("ptraw%d" % hf), res("E")], writes=[res("pt%d" % pi)])
                            ptl[hf].append((pi, nk, vget, kres))
                    return [(0, 2 * kb, ptl[0]), (1, 2 * kb + 1, ptl[1])]

                def pv(allpts, bo, bd):
                    nkt = len(allpts[0][2])
                    for idx in range(nkt):
                        first, lastk = (idx == 0), (idx == nkt - 1)
                        for bank, isden in ((bo, False), (bd, True)):
                            for (hf, g, pts) in allpts:
                                pi, nk, vget, kres = pts[idx]
                                lhs = ones_bf[0:nk, :] if isden else vget(g)
                                sc.op("pe", L("matmul", bank.t[hf * 64:(hf + 1) * 64, 0:W], lhsT=lhs, rhs=pt[0:nk, pi, 0:W], start=first, stop=lastk),
                                      reads=[res("pt%d" % pi), res("ones")] + kres, excl=[bank], signal=(lastk and hf == 1))

                def norm(kb, bo, bd):
                    stg = dsb if kb == 0 else o1
                    sres = "dsb" if kb == 0 else "o1"
                    for r in range(4):
                        j = kb * 4 + r
                        sc.op("act", L("activation", out=stg[:, r * nq:(r + 1) * nq], in_=bd.t[:, r * nq:(r + 1) * nq], func=AF.Ln, bias=esink[:, j:j + 1]),
                              reads=[res("esink")], writes=[res(sres)], excl=[bd])
                    sc.op("act", L("activation", out=stg[:, 0:W], in_=stg[:, 0:W], func=AF.Exp, scale=-1.0), reads=[res(sres)], writes=[res(sres)])
                    sc.op("dve", L("tensor_tensor", out=stg[:, 0:W], in0=bo.t[:, 0:W], in1=stg[:, 0:W], op=ALU.mult), reads=[res(sres)], writes=[res(sres)], excl=[bo])
                    sc.op("pool", L("tensor_tensor", out=a_in[:, kb * 4:(kb + 1) * 4, q0:q0 + nq], in0=stg[:, 0:W].rearrange("p (r q) -> p r q", r=4),
                                    in1=g_ag[:, kb * 4:(kb + 1) * 4, q0:q0 + nq], op=ALU.mult),
                          reads=[res(sres), res("g_ag")], writes=[res("a_in")])

                pk0 = scores_kb(0)
                yield
                pv(pk0, BO, BD)
                yield
                pk1 = scores_kb(1)
                yield
                norm(0, BO, BD)
                bo1, bd1 = next_bank(), next_bank()
                pv(pk1, bo1, bd1)
                norm(1, bo1, bd1)
                yield

        def pool_pre(t):
            n = 128 if t < NT else 32
            for g in range(4):
                w = 2 << g
                b0 = 2 * g
                mres = res("mixed%d" % g)
                if t < NT:
                    LL = 16 + n
                    units = [(lambda lo, hi, b0=b0: pu_sb[:, b0:b0 + 2, lo:hi],
                              lambda lo, hi: sA[:, :, lo:hi],
                              lambda lo, hi: sB[:, :, lo:hi],
                              lambda lo, hi, b0=b0: mixed[:, b0:b0 + 2, lo:hi])]
                    ures = res("pu")
                else:
                    LL = 32
                    units = []
                    for bb in range(2):
                        units.append((lambda lo, hi, bb=bb, b0=b0: pus[:, b0 + bb, :, lo:hi],
                                      lambda lo, hi, bb=bb: sA[:, bb, 0:64].rearrange("p (s c) -> p s c", s=2)[:, :, lo:hi],
                                      lambda lo, hi, bb=bb: sB[:, bb, 0:64].rearrange("p (s c) -> p s c", s=2)[:, :, lo:hi],
                                      lambda lo, hi, bb=bb, b0=b0: mixed[:, b0 + bb, 0:32].rearrange("p (s c) -> p s c", s=2)[:, :, lo:hi]))
                    ures = res("pus")
                for (U, A, B, MX) in units:
                    sc.op("pool", L("tensor_tensor", out=A(2, LL), in0=U(2, LL), in1=U(1, LL - 1), op=ALU.add), reads=[ures], writes=[res("sA")], extra=[c_all, WEV.get("c3")])
                    fin, fres = A, res("sA")
                    if w >= 4:
                        sc.op("pool", L("tensor_tensor", out=B(4, LL), in0=A(4, LL), in1=A(2, LL - 2), op=ALU.add), reads=[res("sA")], writes=[res("sB")])
                        fin, fres = B, res("sB")
                    if w >= 8:
                        sc.op("pool", L("tensor_tensor", out=A(8, LL), in0=B(8, LL), in1=B(4, LL - 4), op=ALU.add), reads=[res("sB")], writes=[res("sA")])
                        fin, fres = A, res("sA")
                    if w >= 16:
                        sc.op("pool", L("tensor_tensor", out=B(16, LL), in0=A(16, LL), in1=A(8, LL - 8), op=ALU.add), reads=[res("sA")], writes=[res("sB")])
                        fin, fres = B, res("sB")
                    sc.op("dve", L("scalar_tensor_tensor", out=MX(0, LL - 16), in0=fin(16, LL), scalar=1.0 / w, in1=U(16, LL), op0=ALU.mult, op1=ALU.subtract),
                          reads=[fres, ures], writes=[mres])
                    if t == 0:
                        sc.op("pool", L("tensor_tensor", out=fin(16, 32), in0=fin(16, 32), in1=invc[:, b0:b0 + 2, :], op=ALU.mult), reads=[fres, mres, res("mT")], writes=[fres], extra=[c_all])
                        sc.op("dve", L("tensor_tensor", out=MX(0, 16), in0=fin(16, 32), in1=U(16, 32), op=ALU.subtract), reads=[fres, ures], writes=[mres])
            if t < NT - 1:
                sc.op("pool", L("tensor_copy", out=pu_sb[:, :, 1:16], in_=pu_sb[:, :, n + 1:n + 16]), reads=[res("pu")], writes=[res("pu")])

        def pool_out(t):
            if t == NT - 1:
                jobs = [(pp, lambda blk: pu_sb[:, blk, 129:144], res("pu"))]
            else:
                jobs = [(spo[bq], (lambda blk, bq=bq: pus[:, blk, bq, 17:32]), res("pus")) for bq in range(2)]
            for (dst, getter, rr) in jobs:
                for half, (stg, sres) in enumerate(((o1, "o1"), (dsb, "dsb"))):
                    b = next_bank()
                    for i in range(4):
                        blk = 4 * half + i
                        sc.op("pe", L("transpose", out=b.t[0:15, i * 128:(i + 1) * 128], in_=getter(blk), identity=identf[:, :]),
                              reads=[rr], excl=[b], extra=[c_all], signal=(i == 3))
                    sc.op("act", L("activation", out=stg[0:15, :], in_=b.t[0:15, :], func=AF.Copy), writes=[res(sres)], excl=[b])
                    sc.dma("sp", L("dma_start", out=dst[:, half * 512:(half + 1) * 512], in_=stg[0:15, :]), "o_pool_" + sres, reads=[res(sres)], is_output=True)
                    yield

        def pool_post(t):
            n = 128 if t < NT else 32
            for gp in range(2):
                bpo = next_bank()
                for gg in range(2):
                    g = 2 * gp + gg
                    b0 = 2 * g
                    for ob in range(2):
                        for kc in range(2):
                            sc.op("pe", L("matmul", bpo.t[:, (2 * gg + ob) * 128:(2 * gg + ob) * 128 + n], lhsT=pw_sb[:, g, kc, ob * 128:(ob + 1) * 128], rhs=mixed[:, b0 + kc, 0:n], start=(kc == 0), stop=(kc == 1)),
                                  reads=[res("mixed%d" % g)], excl=[bpo], extra=[WEV["w_pw"]], signal=(gg == 1 and ob == 1 and kc == 1))
                for i in range(4):
                    blk = 4 * gp + i
                    sc.op("dve", L("scalar_tensor_tensor", out=p_in[:, blk, 0:n], in0=bpo.t[:, i * 128:i * 128 + n], scalar=psc8[:, blk:blk + 1], in1=g_pg[:, blk, 0:n], op0=ALU.mult, op1=ALU.mult),
                          reads=[res("g_pg")], writes=[res("p_in")], excl=[bpo], extra=[c_all])
                yield

        def gate(t):
            n = 128 if t < NT else 32
            for jp in range(4):
                b = next_bank()
                for which, (wsb_, src_, rname, wk) in enumerate(((wa_sb, a_in, "a_in", "w_wa"), (wp_sb, p_in, "p_in", "w_wp"))):
                    for jj in range(2):
                        jb = 2 * jp + jj
                        sl_ = which * 2 + jj
                        for kc in range(8):
                            sc.op("pe", L("matmul", b.t[:, sl_ * 128:sl_ * 128 + n], lhsT=wsb_[:, kc, jb * 128:(jb + 1) * 128], rhs=src_[:, kc, 0:n], start=(kc == 0), stop=(kc == 7)),
                                  reads=[res(rname)], excl=[b], extra=[WEV[wk]], signal=(which == 1 and jj == 1 and kc == 7))
                for jj in range(2):
                    jb = 2 * jp + jj
                    tj = t_ctr[0] % 2
                    t_ctr[0] += 1
                    sc.op("dve", L("scalar_tensor_tensor", out=t1[:, tj, 0:n], in0=th_ma[:, jb, 0:n], scalar=1.0, in1=b.t[:, jj * 128:jj * 128 + n], op0=ALU.add, op1=ALU.mult),
                          reads=[res("th_ma")], writes=[res("t1_%d" % tj)], excl=[b])
                    sc.op("dve", L("scalar_tensor_tensor", out=t2[:, tj, 0:n], in0=th_mp[:, jb, 0:n], scalar=1.0, in1=b.t[:, (2 + jj) * 128:(2 + jj) * 128 + n], op0=ALU.add, op1=ALU.mult),
                          reads=[res("th_mp")], writes=[res("t2_%d" % tj)], excl=[b])
                    sc.op("pool", L("tensor_tensor", out=mT[:, jb, 0:n], in0=t1[:, tj, 0:n], in1=t2[:, tj, 0:n], op=ALU.add),
                          reads=[res("t1_%d" % tj), res("t2_%d" % tj)], writes=[res("mT")])
                yield

        def final(t):
            n = 128 if t < NT else 32
            sl = t % 2
            for nb_, bk in enumerate((BO, BD)):
                for kc in range(8):
                    sc.op("pe", L("matmul", bk.t[0:n, :], lhsT=mT[:, kc, 0:n], rhs=wo_sb[:, kc, nb_ * 512:(nb_ + 1) * 512], start=(kc == 0), stop=(kc == 7)),
                          reads=[res("mT")], excl=[bk], extra=[WEV["w_wo"]], signal=(kc == 7))
                sc.op("dve", L("scalar_tensor_tensor", out=xbuf[0:n, sl, nb_ * 512:(nb_ + 1) * 512], in0=bk.t[0:n, :], scalar=0.5, in1=xbuf[0:n, sl, nb_ * 512:(nb_ + 1) * 512], op0=ALU.mult, op1=ALU.add),
                      reads=[res("xb%d" % sl)], writes=[res("xb%d" % sl)], excl=[bk])
                yield
            dst_ = yp[t * 128:(t + 1) * 128, :] if t < NT else ys
            sc.dma("sp", L("dma_start", out=dst_, in_=xbuf[0:n, sl, :]), "y%d" % sl, reads=[res("xb%d" % sl)], is_output=True)

        def once(fn, *a):
            fn(*a)
            yield

        def interleave(a, b, na=1, nb=1, until_a=False):
            da = db = False
            while not ((da and db) or (until_a and da)):
                for _ in range(na):
                    if not da:
                        try:
                            next(a)
                        except StopIteration:
                            da = True
                for _ in range(nb):
                    if not db:
                        try:
                            next(b)
                        except StopIteration:
                            db = True

        def sample_state():
            sc.dma("pool", L("dma_start", out=stk_tm[:, :, :], in_=sk.rearrange("b t c -> t b c")), "st_k", writes=[res("mixed0"), res("mixed1")])
            sc.dma("pool", L("dma_start", out=vst[:, :, :], in_=sv.rearrange("b t c -> t b c")), "st_v", writes=[res("pu")])
            for bq in range(2):
                for kb in range(2):
                    sc.op("pe", L("transpose", out=psT[:, bq * 2 + kb, :], in_=stk_tm[:, bq, kb * 128:(kb + 1) * 128], identity=ident_bf[:, :]),
                          extra=[WEV["c2"]], reads=[res("mixed0"), res("mixed1")], excl=[BT], signal=(bq == 1 and kb == 1))
            sc.op("act", L("activation", out=kTst[:, :, :, :].rearrange("p a b c -> p (a b c)"), in_=psT[:, 0:4, :].rearrange("p a c -> p (a c)"), func=AF.Copy),
                  writes=[res("pu")], excl=[BT])

        def front1(t):
            last = (t >= NT - 1)
            if t == NT:
                sample_state()
            if dbg.get("pipeqk", 0):
                xprep_b(t)
                sk_ = qk_a(t, "k", [0, 1])
                yield
                sq0 = qk_a(t, "q", [0, 1, 2, 3])
                yield
                qk_b(t, "k", [0, 1], sk_)
                sq1 = qk_a(t, "q", [4, 5, 6, 7])
                yield
                qk_b(t, "q", [0, 1, 2, 3], sq0)
                v_group(t)
                yield
                qk_b(t, "q", [4, 5, 6, 7], sq1)
            else:
                xprep_b(t)
                sk_ = qk_a(t, "k", [0, 1])
                v_group(t)
                qk_b(t, "k", [0, 1], sk_)
                yield
                sq0 = qk_a(t, "q", [0, 1, 2, 3])
                for _ in stage_blocks(t, "pu", only=0):
                    pass
                qk_b(t, "q", [0, 1, 2, 3], sq0)
                yield
                sq1 = qk_a(t, "q", [4, 5, 6, 7])
                for _ in stage_blocks(t, "pu", only=1):
                    pass
                qk_b(t, "q", [4, 5, 6, 7], sq1)
                yield
            if last:
                k_out(t)
            if dbg.get("pipeqk", 0):
                yield from stage_blocks(t, "pu")
            pool_pre(t)
            if last:
                yield from pool_out(t)
            yield from stage_blocks(t, "ag")
            yield from stage_blocks(t, "pg")

        def front2(t):
            yield from stage_blocks(t, "ma")
            yield from pool_post(t)
            yield from stage_blocks(t, "mp")

        def back2(t):
            yield from gate(t)
            yield from final(t)

        def late_setup(i):
            if i < 8:
                for bq in range(2):
                    sc.dma("sp", L("dma_start", out=pus[:, i, bq, 1:16], in_=spl[bq, :, i * 128:(i + 1) * 128].rearrange("t p -> p t"), allow_slow_non_contiguous=True), "c3")
            elif i == 8:
                for bq in range(2):
                    sc.dma("sp", L("dma_start", out=sko[bq, 0:112, :], in_=sk[bq, 16:128, :]), "o_st", is_output=True)
                    sc.dma("sp", L("dma_start", out=svo[bq, 0:112, :], in_=sv[bq, 16:128, :]), "o_st", is_output=True)

        tiles = dbg.get("tiles", list(range(NT + 1)))
        if tiles:
            xprep_a(tiles[0])
            load_perm(C_Q, "q")
            load_perm(C_AG, "ag")
            load_perm(C_MA, "ma", perm=False)
            for _ in front1(tiles[0]):
                pass
            bias_finish()
        for ti, t in enumerate(tiles):
            nxt = tiles[ti + 1] if ti + 1 < len(tiles) else None
            if nxt is not None:
                xload(nxt)
            if ti < 9:
                late_setup(ti)
            if "c3" not in WEV and "c3" in sc.dma_keys and (ti >= 8 or nxt is None or nxt == NT):
                WEV["c3"] = sc.group_total("c3")
            f2 = front2(t)
            att = attention(t)
            done = False
            if dbg.get("oldatt"):
                interleave(att, f2, 2, 1, until_a=True)
                done = True
            while not done:
                for step in ("a", "f", "f", "a"):
                    try:
                        next(att if step == "a" else f2)
                    except StopIteration:
                        if step == "a":
                            done = True
                            break
            if nxt is not None:
                xprep_a(nxt)
            for _ in f2:
                pass
            if nxt is not None:
                b2 = back2(t)
                interleave(front1(nxt), b2, 1, 1)
            else:
                for _ in back2(t):
                    pass

        if dbg.get("dump"):
            for nm, buf, shape, dt_ in (("qT", qT, [128, 8 * 128], BF16), ("kTr", kTr, [128, 4 * 128], BF16), ("vr", vr, [128, 512], BF16),
                                        ("g_ag", g_ag, [128, 1024], BF16), ("g_pg", g_pg, [128, 1024], BF16), ("a_in", a_in, [128, 1024], BF16),
                                        ("p_in", p_in, [128, 1024], BF16), ("mT", mT, [128, 1024], BF16), ("th_ma", th_ma, [128, 1024], BF16),
                                        ("th_mp", th_mp, [128, 1024], BF16), ("pu_sb", pu_sb, [128, 8 * 144], F32), ("hT", hT, [128, 1024], BF16),
                                        ("E", E, [128, 2 * 16 * 128], BF16)):
                dtn = nc.dram_tensor("dbg_" + nm, shape, dt_, kind="ExternalOutput").ap()
                flat = buf
                nd = len(buf.shape)
                if nd == 3:
                    flat = buf[:, :, :].rearrange("p a b -> p (a b)")
                elif nd == 4:
                    flat = buf[:, :, :, :].rearrange("p a b c -> p (a b c)")
                sc.dma("sp", L("dma_start", out=dtn, in_=flat), "dbg", reads=[res(k) for k in list(R.keys())])
        if dbg.get("waitw"):
            for k in dbg["waitw"]:
                sc.op("act", L("activation", out=st[:, 0, 3:4], in_=st[:, 0, 3:4], func=AF.Copy), extra=[WEV[k]])
        sc.emit(block)
    return nc


def _bucket_onehot():
    rel = (127 - np.arange(384)).astype(np.int32)
    nb = 16
    ret = np.where(rel > 0, nb, 0)
    n = np.abs(rel)
    max_exact = nb // 2
    ratio = np.maximum(n, 1).astype(np.float32) / np.float32(max_exact)
    large = max_exact + (np.log(ratio).astype(np.float32) / np.float32(math.log(128 / max_exact))
                         * np.float32(nb - max_exact)).astype(np.int32)
    large = np.minimum(large, nb - 1)
    bucket = ret + np.where(n < max_exact, n, large)
    G = np.zeros((32, 384), np.float32)
    G[bucket, np.arange(384)] = 1.0
    return G


_CACHE = {}


def kernel(x_prompt, x_sample, state_attn_k, state_attn_v, state_pool, norm_gain, w_in,
           q_norm_gain, k_norm_gain, attn_sinks, rel_bias, pool_w, pool_scale,
           w_attn_br, w_pool_br, w_out):
    f = lambda a: np.ascontiguousarray(np.asarray(a, dtype=np.float32))
    if "nc" not in _CACHE:
        _CACHE["nc"] = build_program()
        _CACHE["G"] = _bucket_onehot()
    nc = _CACHE["nc"]
    bd = np.zeros((128, 128), np.float32)
    bd[0:64, 0:64] = 1.0 / 64.0
    bd[64:128, 64:128] = 1.0 / 64.0
    inv = np.zeros((128, 8, 16), np.float32)
    for blk in range(8):
        w = 2 << (blk // 2)
        inv[:, blk, :] = 1.0 / np.minimum(w, np.arange(16) + 1)
    common = {
        "ng": f(norm_gain), "win": f(w_in[0]), "qg": f(q_norm_gain), "kg": f(k_norm_gain),
        "sinks": f(attn_sinks), "rb": f(rel_bias), "pw": f(pool_w[0]), "psc": f(pool_scale),
        "wa": f(w_attn_br[0]), "wp": f(w_pool_br[0]), "wo": f(w_out[0]),
        "cG": _CACHE["G"], "cI": np.eye(128, dtype=np.float32), "cBD": bd, "cINV": inv.reshape(128, 128),
    }
    x_prompt = f(x_prompt)
    x_sample = f(x_sample)
    sk_ = f(state_attn_k)[0].reshape(16, 128, 256)
    sv_ = f(state_attn_v)[0].reshape(16, 128, 256)
    spl_ = f(state_pool)[0]
    in_maps = []
    for c in range(NCORES):
        m = dict(common)
        m["xp"] = x_prompt[c]
        m["xs"] = np.ascontiguousarray(x_sample[2 * c:2 * c + 2].reshape(32, D))
        m["sk"] = np.ascontiguousarray(sk_[2 * c:2 * c + 2])
        m["sv"] = np.ascontiguousarray(sv_[2 * c:2 * c + 2])
        m["spl"] = np.ascontiguousarray(spl_[2 * c:2 * c + 2])
        in_maps.append(m)
    res = run_bass_kernel_spmd(nc, in_maps, core_ids=list(range(NCORES)))
    r = res.results
    y_p = np.stack([r[c]["yp"] for c in range(NCORES)], 0)
    y_s = np.concatenate([r[c]["ys"].reshape(2, 16, D) for c in range(NCORES)], 0)
    p_k = np.stack([r[c]["pk"].reshape(128, 4, 64) for c in range(NCORES)], 0)[None]
    p_v = np.stack([r[c]["pv"].reshape(128, 4, 64) for c in range(NCORES)], 0)[None]
    p_p = np.stack([r[c]["pp"] for c in range(NCORES)], 0)[None]
    s_k = np.concatenate([r[c]["sko"].reshape(2, 128, 4, 64) for c in range(NCORES)], 0)[None]
    s_v = np.concatenate([r[c]["svo"].reshape(2, 128, 4, 64) for c in range(NCORES)], 0)[None]
    s_p = np.concatenate([r[c]["spo"] for c in range(NCORES)], 0)[None]
    return (y_p.astype(np.float32), y_s.astype(np.float32), p_k.astype(np.float32), p_v.astype(np.float32),
            p_p.astype(np.float32), s_k.astype(np.float32), s_v.astype(np.float32), s_p.astype(np.float32))
```

```python
from contextlib import ExitStack
import math
import numpy as np
import concourse.bass as bass
import concourse.mybir as mybir
from concourse.bass_utils import run_bass_kernel_spmd

F32 = mybir.dt.float32
BF16 = mybir.dt.bfloat16
ALU = mybir.AluOpType
AF = mybir.ActivationFunctionType

NCORES = 8
SEQ = 2048
D = 1024
NT = SEQ // 128
EPS = 1e-6
C_Q, C_K, C_V, C_AG, C_PU, C_PG, C_MA, C_MP = 0, 1024, 1280, 1536, 2560, 3584, 4608, 5632


class Ev:
    __slots__ = ("sem", "value")

    def __init__(self, sem, value):
        self.sem = sem
        self.value = value


class Res:
    __slots__ = ("name", "w", "r")

    def __init__(self, name):
        self.name = name
        self.w = None
        self.r = []


class Bank:
    def __init__(self, name, t):
        self.name = name
        self.t = t
        self.last = {}


class Stream:
    def __init__(self, name, sem):
        self.name = name
        self.sem = sem
        self.count = 0
        self.ops = []
        self.waited = {}


class Sched:
    def __init__(self, nc, es):
        self.nc = nc
        self.es = es
        self.streams = {}
        for n in ("pe", "act", "dve", "pool", "sp"):
            sem = es.enter_context(nc.semaphore("s_" + n))
            self.streams[n] = Stream(n, sem)
        self.dma_keys = {}
        self.out_events = []

    def _deps(self, reads, writes, extra):
        deps = [e for e in extra if e is not None]
        for r in reads:
            if r.w is not None:
                deps.append(r.w)
        for w in writes:
            if w.w is not None:
                deps.append(w.w)
            deps.extend(w.r)
        return deps

    def _waits(self, st, deps):
        best = {}
        for d in deps:
            k = id(d.sem)
            if k not in best or best[k].value < d.value:
                best[k] = d
        waits = []
        for k, d in best.items():
            if st.waited.get(k, -1) >= d.value:
                continue
            st.waited[k] = d.value
            waits.append(d)
        return waits

    def _commit(self, ev, reads, writes):
        for r in reads:
            r.r.append(ev)
        for w in writes:
            w.w = ev
            w.r = []

    def op(self, stream, fn, reads=(), writes=(), extra=(), signal=True, excl=()):
        st = self.streams[stream]
        deps = self._deps(reads, writes, extra)
        for b in excl:
            deps.extend(ev for s, ev in b.last.items() if s != stream)
        if stream == "pe":
            deps = [d for d in deps if d.sem is not st.sem]
        waits = self._waits(st, deps)
        if not signal:
            st.ops.append((waits, fn, None))
            return None
        st.count += 1
        ev = Ev(st.sem, st.count)
        st.ops.append((waits, fn, (st.sem, 1)))
        self._commit(ev, reads, writes)
        for b in excl:
            b.last[stream] = ev
        return ev

    def dma(self, stream, fn, key, reads=(), writes=(), extra=(), is_output=False):
        st = self.streams[stream]
        if key not in self.dma_keys:
            sem = self.es.enter_context(self.nc.semaphore("d_" + key))
            self.dma_keys[key] = [sem, 0]
        ent = self.dma_keys[key]
        deps = [d for d in self._deps(reads, writes, extra) if d.sem is not ent[0]]
        waits = self._waits(st, deps)
        ent[1] += 16
        ev = Ev(ent[0], ent[1])
        st.ops.append((waits, fn, (ent[0], 16)))
        self._commit(ev, reads, writes)
        if is_output:
            self.out_events.append(ev)
        return ev

    def group_total(self, key):
        ent = self.dma_keys[key]
        return Ev(ent[0], ent[1])

    def emit(self, block):
        sch = self

        def run(st, eng):
            for waits, fn, inc in st.ops:
                for w in waits:
                    eng.wait_ge(w.sem, w.value)
                ins = fn(eng)
                if inc is not None:
                    ins.then_inc(inc[0], inc[1])

        finals = [Ev(ent[0], ent[1]) for ent in self.dma_keys.values()]

        @block.sync
        def _(e):
            run(sch.streams["sp"], e)
            for ev in finals:
                e.wait_ge(ev.sem, ev.value)

        @block.scalar
        def _(e):
            run(sch.streams["act"], e)

        @block.vector
        def _(e):
            run(sch.streams["dve"], e)

        @block.gpsimd
        def _(e):
            run(sch.streams["pool"], e)

        @block.tensor
        def _(e):
            run(sch.streams["pe"], e)


def L(f, *args, **kw):
    return lambda e: getattr(e, f)(*args, **kw)


def build_program(dbg=None):
    dbg = dbg or {}
    nc = bass.Bass("TRN2", target_bir_lowering=False, dynamic_dma_scratch_size=2048)

    def din(name, shape):
        return nc.dram_tensor(name, shape, F32, kind="ExternalInput").ap()

    def dout(name, shape):
        return nc.dram_tensor(name, shape, F32, kind="ExternalOutput").ap()

    xp = din("xp", [SEQ, D])
    xs = din("xs", [32, D])
    sk = din("sk", [2, 128, 256])
    sv = din("sv", [2, 128, 256])
    spl = din("spl", [2, 15, D])
    ng = din("ng", [1, D])
    win = din("win", [D, 6656])
    qg = din("qg", [1, 64])
    kg = din("kg", [1, 64])
    sinks = din("sinks", [1, 16])
    rb = din("rb", [32, 16])
    pw = din("pw", [4, 256, 256])
    psc = din("psc", [1, D])
    wa = din("wa", [D, D])
    wp = din("wp", [D, D])
    wo = din("wo", [D, D])
    cG = din("cG", [32, 384])
    cI = din("cI", [128, 128])
    cBD = din("cBD", [128, 128])
    cINV = din("cINV", [128, 8 * 16])

    yp = dout("yp", [SEQ, D])
    ys = dout("ys", [32, D])
    pk = dout("pk", [128, 256])
    pv = dout("pv", [128, 256])
    pp = dout("pp", [15, D])
    sko = dout("sko", [2, 128, 256])
    svo = dout("svo", [2, 128, 256])
    spo = dout("spo", [2, 15, D])
    scr = nc.dram_tensor("scr", [16, 128 * 384], F32, kind="Internal").ap()

    with ExitStack() as es:
        def sb(name, shape, dt):
            return es.enter_context(nc.sbuf_tensor(name, shape, dt))

        def ps(name, shape, dt):
            return es.enter_context(nc.psum_tensor(name, shape, dt))

        win_sb = sb("win_sb", [128, 8, 6656], BF16)
        wa_sb = sb("wa_sb", [128, 8, D], BF16)
        wp_sb = sb("wp_sb", [128, 8, D], BF16)
        wo_sb = sb("wo_sb", [128, 8, D], BF16)
        pw_sb = sb("pw_sb", [128, 4, 2, 256], BF16)
        xbuf = sb("xbuf", [128, 2, D], F32)
        hp = sb("hp", [128, D], BF16)
        hT = sb("hT", [128, 8, 128], BF16)
        gain_bc = sb("gain_bc", [128, D], BF16)
        E = sb("E", [128, 2, 16, 128], BF16)
        qT = sb("qT", [128, 8, 128], BF16)
        kTr = sb("kTr", [128, 2, 2, 128], BF16)
        vr = sb("vr", [128, 2, 256], BF16)
        ptraw = sb("ptraw", [128, 2, 512], BF16)
        pt = sb("pt", [128, 4, 512], BF16)
        dsb = sb("dsb", [128, 512], F32)
        o1 = sb("o1", [128, 512], F32)
        g_ag = sb("g_ag", [128, 8, 128], BF16)
        g_pg = sb("g_pg", [128, 8, 128], BF16)
        a_in = sb("a_in", [128, 8, 128], BF16)
        p_in = sb("p_in", [128, 8, 128], BF16)
        mT = sb("mT", [128, 8, 128], BF16)
        th_ma = sb("th_ma", [128, 8, 128], BF16)
        th_mp = sb("th_mp", [128, 8, 128], BF16)
        pu_sb = sb("pu_sb", [128, 8, 144], F32)
        sA = sb("sA", [128, 2, 144], F32)
        sB = sb("sB", [128, 2, 144], F32)
        mixed = sb("mixed", [128, 8, 128], BF16)
        t1 = sb("t1", [128, 2, 128], F32)
        t2 = sb("t2", [128, 2, 128], F32)
        pus = sb("pus", [128, 8, 2, 32], F32)
        ident_bf = sb("ident_bf", [128, 128], BF16)
        identf = sb("identf", [128, 128], F32)
        bd_bf = sb("bd_bf", [128, 128], BF16)
        ones_bf = sb("ones_bf", [128, 64], BF16)
        qgcol = sb("qgcol", [128, 1], F32)
        kgcol = sb("kgcol", [128, 1], F32)
        sink_t = sb("sink_t", [128, 8], F32)
        esink = sb("esink", [128, 8], F32)
        psc8 = sb("psc8", [128, 8], F32)
        st = sb("st", [128, 2, 4], F32)
        kf32 = o1[:, 0:256].rearrange("p (a b) -> p a b", a=2)
        kout = o1[:, 256:512]
        sq = ptraw
        rs = dsb
        G_sb = o1[0:32, 0:384]
        pu_bf = pu_sb[:, :, :].rearrange("p a b -> p (a b)").bitcast(BF16)
        kTst = pu_bf[:, 0:512].rearrange("p (a b c) -> p a b c", a=2, b=2)
        vst = pu_bf[:, 512:1024].rearrange("p (a c) -> p a c", a=2)
        stk_tm = mixed[:, 0:4, :].rearrange("p (a b) c -> p a (b c)", a=2)
        invc = mT[:, 0:2, :].bitcast(F32).rearrange("p a (b c) -> p (a b) c", c=16)
        tb = o1[0:32, 384:400]
        S_sb = dsb[0:16, 0:384]

        psT = ps("psT", [128, 8, 128], BF16)
        pbt = [ps("pb%d" % i, [128, 512], F32) for i in range(7)]
        BT = Bank("psT", psT)
        BB = [Bank("bb%d" % i, pbt[i]) for i in range(3)]
        BS = [Bank("bs%d" % i, pbt[3 + i]) for i in range(2)]
        BO = Bank("bo", pbt[5])
        BD = Bank("bd", pbt[6])
        ps_o, ps_d = BO.t, BD.t

        sc = Sched(nc, es)
        R = {}

        def res(n):
            if n not in R:
                R[n] = Res(n)
            return R[n]

        block = es.enter_context(nc.Block())

        def cdma(out, in_, wr, slow=False):
            if slow:
                return sc.dma("sp", L("dma_start", out=out, in_=in_, allow_slow_non_contiguous=True), "c", writes=[res(wr)])
            return sc.dma("sp", L("dma_start", out=out, in_=in_), "c", writes=[res(wr)])

        _t0 = dbg.get("tiles", [0])
        if _t0:
            _n0 = 128 if _t0[0] < NT else 32
            _src0 = xp[_t0[0] * 128:(_t0[0] + 1) * 128, :] if _t0[0] < NT else xs
            sc.dma("sp", L("dma_start", out=xbuf[0:_n0, _t0[0] % 2, :], in_=_src0), "x%d" % (_t0[0] % 2), writes=[res("xb%d" % (_t0[0] % 2))])
        cdma(tb, rb, "o1")
        cdma(G_sb, cG, "o1")
        cdma(identf[:, :], cI, "identf")
        for hf in range(2):
            cdma(qgcol[hf * 64:(hf + 1) * 64, :], qg.rearrange("o d -> d o"), "qgcol%d" % hf)
            cdma(kgcol[hf * 64:(hf + 1) * 64, :], kg.rearrange("o d -> d o"), "kgcol%d" % hf)
            cdma(sink_t[hf * 64:(hf + 1) * 64, :].rearrange("p (a b) -> p a b", a=2),
                 bass.AP(tensor=sinks.tensor, offset=4 * hf, ap=[[0, 64], [8, 2], [1, 4]]), "sink%d" % hf)
        cdma(psc8[:, :], psc.rearrange("o (b p) -> p (o b)", p=128), "psc8", slow=True)
        cdma(invc, cINV.rearrange("p (a b) -> p a b", a=8), "mT")
        R["qgcol"] = res("qgcol1")
        R["kgcol"] = res("kgcol1")
        c_all = None

        sc.op("pool", L("memset", ones_bf[:, :], 1.0), writes=[res("ones")])
        sc.op("pool", L("memset", pu_sb[:, :, 0:16], 0.0), writes=[res("pu")])
        sc.op("pool", L("memset", st[:, :, :], 0.0), writes=[res("st0"), res("st1")])
        def wdma(out, in_, key):
            sc.dma("pool", L("dma_start", out=out, in_=in_), key)

        wdma(ident_bf[:, :], cI, "c2")
        wdma(bd_bf[:, :], cBD, "c2")
        wdma(gain_bc[:, :], bass.AP(tensor=ng.tensor, offset=0, ap=[[0, 128], [1, D]]), "c2")

        def w_nat(wsb, wsrc, c0, C, key):
            wdma(wsb[:, :, c0:c0 + C], wsrc[:, c0:c0 + C].rearrange("(kc p) c -> p kc c", p=128), key)

        w_nat(win_sb, win, C_K, 512, "w_kv")
        for i in range(2):
            w_nat(win_sb, win, C_PU + 512 * i, 512, "w_pu")
        for i in range(2):
            w_nat(win_sb, win, C_PG + 512 * i, 512, "w_pg")
        for i in range(2):
            w_nat(win_sb, win, C_MP + 512 * i, 512, "w_mp")
        wdma(pw_sb[:, :, :, :], pw.rearrange("g (kc p) c -> p g kc c", p=128), "w_pw")
        for kb in range(2):
            for hf in range(2):
                wdma(wa_sb[hf * 64:(hf + 1) * 64, kb * 4:(kb + 1) * 4, :],
                     wa[512 * kb + 256 * hf:512 * kb + 256 * hf + 256, :].rearrange("(r d) c -> d r c", d=64), "w_wa")
        for i in range(2):
            w_nat(wp_sb, wp, 512 * i, 512, "w_wp")
        for i in range(2):
            w_nat(wo_sb, wo, 512 * i, 512, "w_wo")
        WEV = {k: sc.group_total(k) for k in ["c2", "w_kv", "w_pu", "w_pg", "w_pw", "w_mp", "w_wa", "w_wp", "w_wo"]}
        WEV["w_v"] = WEV["w_kv"]
        c_all = sc.group_total("c")

        sc.op("act", L("activation", out=esink[:, :], in_=sink_t[:, :], func=AF.Exp), extra=[c_all], writes=[res("esink")])

        sc.op("pe", L("matmul", BB[0].t[0:16, 0:384], lhsT=tb, rhs=G_sb, start=True, stop=True), extra=[c_all], reads=[res("o1")], excl=[BB[0]])
        sc.op("act", L("activation", out=S_sb, in_=BB[0].t[0:16, 0:384], func=AF.Copy), writes=[res("dsb")], excl=[BB[0]])
        sc.dma("sp", L("dma_start", out=scr.rearrange("h (r i) -> h r i", i=384), in_=S_sb.unsqueeze(1).to_broadcast([16, 128, 384])), "scrw", reads=[res("dsb")], writes=[res("scr")])
        def bias_finish():
            for ab, c0 in ((0, 255), (1, 127)):
                for hh in range(2):
                    sc.dma("sp", L("dma_start", out=xbuf[:, 1, :].rearrange("p (h q) -> p h q", h=8),
                                   in_=bass.AP(tensor=scr.tensor, offset=c0 + hh * 8 * 128 * 384, ap=[[383, 128], [128 * 384, 8], [1, 128]])),
                           "toe", reads=[res("scr")], writes=[res("xb1")])
                    sc.op("act", L("activation", out=E[:, ab, hh * 8:(hh + 1) * 8, :].rearrange("p h q -> p (h q)"), in_=xbuf[:, 1, :], func=AF.Exp), reads=[res("xb1")], writes=[res("E")])
            sc.op("dve", L("memset", E[0:64, 0, :, 64:128], 0.0), writes=[res("E")])
            sc.op("dve", L("memset", E[64:128, 1, :, 0:64], 0.0), writes=[res("E")])


        stage_bufs = [(th_ma, "th_ma"), (th_mp, "th_mp"), (p_in, "p_in"), (g_pg, "g_pg")]
        stage_ctr = [0]
        WQ = {}

        def load_perm(c0, name, perm=True):
            rounds = [(kb, kc) for kb in range(2) for kc in range(8)]
            slots = []

            def issue(i):
                kb, kc = rounds[i]
                buf, rn = stage_bufs[stage_ctr[0] % 4]
                stage_ctr[0] += 1
                stg = buf[:, :, :].rearrange("p a b -> p (a b)").bitcast(F32)
                sc.dma("act", L("dma_start", out=stg, in_=win[kc * 128:(kc + 1) * 128, c0 + kb * 512:c0 + (kb + 1) * 512]), "wst_" + rn, writes=[res(rn)])
                slots.append((stg, rn))

            for i in range(4):
                issue(i)
            ev = None
            for i, (kb, kc) in enumerate(rounds):
                stg, rn = slots[i]
                if perm:
                    ev = sc.op("act", L("activation", out=win_sb[:, kc, c0 + kb * 512:c0 + (kb + 1) * 512].rearrange("p (r hf d) -> p r hf d", r=4, hf=2),
                                        in_=stg.rearrange("p (hf r d) -> p r hf d", hf=2, r=4), func=AF.Copy), reads=[res(rn)])
                else:
                    ev = sc.op("act", L("activation", out=win_sb[:, kc, c0 + kb * 512:c0 + (kb + 1) * 512], in_=stg, func=AF.Copy), reads=[res(rn)])
                if i + 4 < len(rounds):
                    issue(i + 4)
            WQ[name] = ev


        bb_ctr = [0]

        ROT = BB + BS

        held = set()

        def next_bank():
            while True:
                b = ROT[bb_ctr[0] % len(ROT)]
                bb_ctr[0] += 1
                if b.name not in held:
                    return b

        pt_ctr = [0]
        sq_ctr = [0]
        s_ctr = [0]
        t_ctr = [0]

        def V(ap2d, nb, n):
            return ap2d[:, 0:nb * 128].rearrange("p (b c) -> p b c", c=128)[:, :, 0:n]

        xloaded = set(dbg.get("tiles", [0])[:1])

        def xload(t):
            sl = t % 2
            n = 128 if t < NT else 32
            src_ = xp[t * 128:(t + 1) * 128, :] if t < NT else xs
            sc.dma("sp", L("dma_start", out=xbuf[0:n, sl, :], in_=src_), "x%d" % sl, writes=[res("xb%d" % sl)])
            xloaded.add(t)

        def xprep_a(t):
            sl = t % 2
            n = 128 if t < NT else 32
            src_ = xp[t * 128:(t + 1) * 128, :] if t < NT else xs
            if not (dbg.get("noprefetch") is None and t in xloaded):
                sc.dma("sp", L("dma_start", out=xbuf[0:n, sl, :], in_=src_), "x%d" % sl, writes=[res("xb%d" % sl)])
            sc.op("act", L("activation", out=hp[0:n, :], in_=xbuf[0:n, sl, :], func=AF.Square, scale=1.0 / 32.0, accum_out=st[0:n, sl, 0:1]),
                  reads=[res("xb%d" % sl)], writes=[res("hp"), res("st%d" % sl)])
            sc.op("act", L("activation", out=st[0:n, sl, 1:2], in_=st[0:n, sl, 0:1], func=AF.Ln, bias=EPS), reads=[res("st%d" % sl)], writes=[res("st%d" % sl)])
            sc.op("act", L("activation", out=st[0:n, sl, 2:3], in_=st[0:n, sl, 1:2], func=AF.Exp, scale=-0.5), reads=[res("st%d" % sl)], writes=[res("st%d" % sl)])
            sc.op("dve", L("scalar_tensor_tensor", out=hp[0:n, :], in0=xbuf[0:n, sl, :], scalar=st[0:n, sl, 2:3], in1=gain_bc[0:n, :], op0=ALU.mult, op1=ALU.mult),
                  reads=[res("xb%d" % sl), res("st%d" % sl)], writes=[res("hp")], extra=[WEV["c2"]])
            sc.op("pool", L("memset", st[0:n, sl, 0:1], 0.0), reads=[], writes=[res("st%d" % sl)])

        def xprep_b(t):
            n = 128 if t < NT else 32
            for kc in range(8):
                sc.op("pe", L("transpose", out=psT[:, kc, 0:n], in_=hp[0:n, kc * 128:(kc + 1) * 128], identity=ident_bf[0:n, 0:n]),
                      reads=[res("hp")], excl=[BT], extra=[WEV["c2"]], signal=(kc == 7))
            sc.op("act", L("activation", out=hT[:, :, 0:n], in_=psT[:, :, 0:n], func=AF.Copy), writes=[res("hT")], excl=[BT])

        def mm_group(n, cols, wev):
            b = next_bank()
            nb = len(cols)
            for bi, col0 in enumerate(cols):
                for kc in range(8):
                    sc.op("pe", L("matmul", b.t[:, bi * 128:bi * 128 + n], lhsT=win_sb[:, kc, col0:col0 + 128], rhs=hT[:, kc, 0:n], start=(kc == 0), stop=(kc == 7)),
                          reads=[res("hT")], excl=[b], extra=[wev], signal=(bi == nb - 1 and kc == 7))
            return b

        def qk_a(t, typ, idxs):
            n = 128 if t < NT else 32
            nb = len(idxs)
            base = C_K if typ == "k" else C_Q
            b = mm_group(n, [base + 128 * i for i in idxs], WEV["w_kv"] if typ == "k" else WQ["q"])
            sqs = sq_ctr[0] % 2
            sq_ctr[0] += 1
            sc.op("act", L("activation", out=V(sq[:, sqs, :], nb, n), in_=V(b.t, nb, n), func=AF.Square), writes=[res("ptraw%d" % sqs)], excl=[b])
            held.add(b.name)
            return (b, sqs)

        def qk_b(t, typ, idxs, st_):
            b, sqs = st_
            n = 128 if t < NT else 32
            par = t % 2
            last = (t >= NT - 1)
            nb = len(idxs)
            b2 = next_bank()
            sqv = sq[:, sqs, :]
            if n == 128:
                sc.op("pe", L("matmul", V(b2.t, nb, n), lhsT=bd_bf[:, :], rhs=V(sqv, nb, n), start=True, stop=True), reads=[res("ptraw%d" % sqs)], excl=[b2], extra=[WEV["c2"]])
            else:
                for bi_ in range(nb):
                    sc.op("pe", L("matmul", b2.t[:, bi_ * 128:bi_ * 128 + n], lhsT=bd_bf[:, :], rhs=sqv[:, bi_ * 128:bi_ * 128 + n], start=True, stop=True),
                          reads=[res("ptraw%d" % sqs)], excl=[b2], extra=[WEV["c2"]], signal=(bi_ == nb - 1))
            sc.op("act", L("activation", out=V(rs, nb, n), in_=V(b2.t, nb, n), func=AF.Ln, bias=EPS), writes=[res("dsb")], excl=[b2])
            sc.op("act", L("activation", out=V(rs, nb, n), in_=V(rs, nb, n), func=AF.Exp, scale=-0.5), reads=[res("dsb")], writes=[res("dsb")])
            if typ == "k":
                dst, dres, gcol = kTr[:, :, par, 0:n], res("kT%d" % par), kgcol
            else:
                dst, dres, gcol = qT[:, idxs[0]:idxs[0] + nb, 0:n], res("qT"), qgcol
            sc.op("dve", L("scalar_tensor_tensor", out=dst, in0=V(b.t, nb, n), scalar=gcol[:, 0:1], in1=V(rs, nb, n), op0=ALU.mult, op1=ALU.mult),
                  reads=[res("dsb")], writes=[dres], excl=[b], extra=[c_all])
            if typ == "k" and last:
                sc.op("dve", L("scalar_tensor_tensor", out=kf32[:, :, 0:n], in0=V(b.t, nb, n), scalar=gcol[:, 0:1], in1=V(rs, nb, n), op0=ALU.mult, op1=ALU.mult),
                      reads=[res("dsb")], writes=[res("o1")], excl=[b])
            held.discard(b.name)

        def v_group(t):
            n = 128 if t < NT else 32
            par = t % 2
            last = (t >= NT - 1)
            if t < NT:
                b = next_bank()
                for kc in range(8):
                    sc.op("pe", L("matmul", b.t[0:n, 0:256], lhsT=hT[:, kc, 0:n], rhs=win_sb[:, kc, C_V:C_V + 256], start=(kc == 0), stop=(kc == 7)),
                          reads=[res("hT")], excl=[b], extra=[WEV["w_v"]], signal=(kc == 7))
                sc.op("act", L("activation", out=vr[0:n, par, :], in_=b.t[0:n, 0:256], func=AF.Copy), writes=[res("vr%d" % par)], excl=[b])
                if last:
                    vo = t1[:, :, :].rearrange("p a b -> p (a b)")
                    sc.op("act", L("activation", out=vo[0:n, :], in_=b.t[0:n, 0:256], func=AF.Copy), writes=[res("t1_0"), res("t1_1")], excl=[b])
                    sc.dma("sp", L("dma_start", out=pv, in_=vo[0:n, :]), "o_v0", reads=[res("t1_0"), res("t1_1")], is_output=True)
            else:
                for bq in range(2):
                    b = next_bank()
                    for kc in range(8):
                        sc.op("pe", L("matmul", b.t[0:16, 0:256], lhsT=hT[:, kc, bq * 16:(bq + 1) * 16], rhs=win_sb[:, kc, C_V:C_V + 256], start=(kc == 0), stop=(kc == 7)),
                              reads=[res("hT")], excl=[b], extra=[WEV["w_v"]], signal=(kc == 7))
                    sc.op("act", L("activation", out=vr[0:16, bq, :], in_=b.t[0:16, 0:256], func=AF.Copy), writes=[res("vr%d" % bq)], excl=[b])
                    tt = t1 if bq == 0 else t2
                    vo = tt[:, :, :].rearrange("p a b -> p (a b)")
                    tr = [res("t%d_0" % (bq + 1)), res("t%d_1" % (bq + 1))]
                    sc.op("act", L("activation", out=vo[0:16, :], in_=b.t[0:16, 0:256], func=AF.Copy), writes=tr, excl=[b])
                    sc.dma("sp", L("dma_start", out=svo[bq, 112:128, :], in_=vo[0:16, :]), "o_v%d" % bq, reads=tr, is_output=True)

        def k_out(t):
            n = 128 if t < NT else 32
            b = next_bank()
            for kb in range(2):
                sc.op("pe", L("transpose", out=b.t[0:n, kb * 128:(kb + 1) * 128], in_=kf32[:, kb, 0:n], identity=identf[:, :]),
                      reads=[res("o1")], excl=[b], extra=[c_all], signal=(kb == 1))
            sc.op("act", L("activation", out=kout[0:n, :], in_=b.t[0:n, 0:256], func=AF.Copy), writes=[res("o1")], excl=[b])
            if t < NT:
                sc.dma("sp", L("dma_start", out=pk, in_=kout[0:n, :]), "o_k", reads=[res("o1")], is_output=True)
            else:
                for bq in range(2):
                    sc.dma("sp", L("dma_start", out=sko[bq, 112:128, :], in_=kout[bq * 16:(bq + 1) * 16, :]), "o_k", reads=[res("o1")], is_output=True)

        def stage_blocks(t, typ, only=None):
            n = 128 if t < NT else 32
            for gi in (range(2) if only is None else [only]):
                idxs = [4 * gi + i for i in range(4)]
                if typ == "pu":
                    b = mm_group(n, [C_PU + 128 * i for i in idxs], WEV["w_pu"])
                    if t < NT:
                        sc.op("act", L("activation", out=pu_sb[:, 4 * gi:4 * gi + 4, 16:16 + n], in_=V(b.t, 4, n), func=AF.Copy), writes=[res("pu")], excl=[b])
                    else:
                        for i in range(4):
                            sc.op("act", L("activation", out=pus[:, 4 * gi + i, :, 16:32], in_=b.t[:, i * 128:i * 128 + 32].rearrange("p (b c) -> p b c", b=2), func=AF.Copy),
                                  writes=[res("pus")], excl=[b], extra=[c_all])
                elif typ in ("ag", "pg"):
                    c0, wev_, dst = (C_AG, WQ["ag"], g_ag) if typ == "ag" else (C_PG, WEV["w_pg"], g_pg)
                    b = mm_group(n, [c0 + 128 * i for i in idxs], wev_)
                    sc.op("act", L("activation", out=dst[:, 4 * gi:4 * gi + 4, 0:n], in_=V(b.t, 4, n), func=AF.Silu), writes=[res("g_" + typ)], excl=[b])
                else:
                    c0, wev_, dst = (C_MA, WQ["ma"], th_ma) if typ == "ma" else (C_MP, WEV["w_mp"], th_mp)
                    b = mm_group(n, [c0 + 128 * i for i in idxs], wev_)
                    sc.op("act", L("activation", out=dst[:, 4 * gi:4 * gi + 4, 0:n], in_=V(b.t, 4, n), func=AF.Tanh, scale=0.5), writes=[res("th_" + typ)], excl=[b])
                yield

        def attention(t):
            par = t % 2
            if t < NT:
                seqs = [(0, 128)]
            else:
                seqs = [(0, 16), (16, 16)]
            for bi, (q0, nq) in enumerate(seqs):
                if t < NT:
                    ktiles = []
                    if t > 0:
                        ktiles.append((lambda hf, kb: kTr[hf * 64:(hf + 1) * 64, kb, 1 - par, 0:128], lambda g: vr[0:128, 1 - par, g * 64:(g + 1) * 64], 128, 0,
                                       [res("kT%d" % (1 - par)), res("vr%d" % (1 - par))]))
                    ktiles.append((lambda hf, kb: kTr[hf * 64:(hf + 1) * 64, kb, par, 0:128], lambda g: vr[0:128, par, g * 64:(g + 1) * 64], 128, 1,
                                   [res("kT%d" % par), res("vr%d" % par)]))
                else:
                    ktiles = [(lambda hf, kb, bi=bi: kTst[hf * 64:(hf + 1) * 64, bi, kb, :], lambda g, bi=bi: vst[0:128, bi, g * 64:(g + 1) * 64], 128, 0, [res("pu")]),
                              (lambda hf, kb, q0=q0: kTr[hf * 64:(hf + 1) * 64, kb, par, q0:q0 + 16], lambda g, bi=bi: vr[0:16, bi, g * 64:(g + 1) * 64], 16, 1,
                               [res("kT%d" % par), res("vr%d" % bi)])]
                W = 4 * nq

                def scores_kb(kb):
                    ptl = {0: [], 1: []}
                    for (kget, vget, nk, ab, kres) in ktiles:
                        banks = [next_bank(), next_bank()]
                        for hf in range(2):
                            sc.op("pe", L("matmul", banks[hf].t[0:nk, 0:W].rearrange("p (r q) -> p r q", r=4),
                                          lhsT=kget(hf, kb), rhs=qT[hf * 64:(hf + 1) * 64, kb * 4:(kb + 1) * 4, q0:q0 + nq], start=True, stop=True),
                                  reads=[res("qT"), kres[0]], excl=[banks[hf]])
                        for hf in range(2):
                            sc.op("act", L("activation", out=ptraw[0:nk, hf, 0:W], in_=banks[hf].t[0:nk, 0:W], func=AF.Exp, scale=0.125),
                                  writes=[res("ptraw%d" % hf)], excl=[banks[hf]])
                        for hf in range(2):
                            g = 2 * kb + hf
                            pi = pt_ctr[0] % 4
                            pt_ctr[0] += 1
                            sc.op("dve", L("tensor_tensor", out=pt[0:nk, pi, 0:W].rearrange("p (r q) -> p r q", r=4),
                                           in0=ptraw[0:nk, hf, 0:W].rearrange("p (r q) -> p r q", r=4),
                                           in1=E[0:nk, ab, g * 4:(g + 1) * 4, 0:nq], op=ALU.mult),
                                  reads=[res("ptraw%d" % hf), res("E")], writes=[res("pt%d" % pi)])
                            ptl[hf].append((pi, nk, vget, kres))
                    return [(0, 2 * kb, ptl[0]), (1, 2 * kb + 1, ptl[1])]

                def pv(allpts, bo, bd):
                    nkt = len(allpts[0][2])
                    for idx in range(nkt):
                        first, lastk = (idx == 0), (idx == nkt - 1)
                        for bank, isden in ((bo, False), (bd, True)):
                            for (hf, g, pts) in allpts:
                                pi, nk, vget, kres = pts[idx]
                                lhs = ones_bf[0:nk, :] if isden else vget(g)
                                sc.op("pe", L("matmul", bank.t[hf * 64:(hf + 1) * 64, 0:W], lhsT=lhs, rhs=pt[0:nk, pi, 0:W], start=first, stop=lastk),
                                      reads=[res("pt%d" % pi), res("ones")] + kres, excl=[bank], signal=(lastk and hf == 1))

                def norm(kb, bo, bd):
                    stg = dsb if kb == 0 else o1
                    sres = "dsb" if kb == 0 else "o1"
                    for r in range(4):
                        j = kb * 4 + r
                        sc.op("act", L("activation", out=stg[:, r * nq:(r + 1) * nq], in_=bd.t[:, r * nq:(r + 1) * nq], func=AF.Ln, bias=esink[:, j:j + 1]),
                              reads=[res("esink")], writes=[res(sres)], excl=[bd])
                    sc.op("act", L("activation", out=stg[:, 0:W], in_=stg[:, 0:W], func=AF.Exp, scale=-1.0), reads=[res(sres)], writes=[res(sres)])
                    sc.op("dve", L("tensor_tensor", out=stg[:, 0:W], in0=bo.t[:, 0:W], in1=stg[:, 0:W], op=ALU.mult), reads=[res(sres)], writes=[res(sres)], excl=[bo])
                    sc.op("pool", L("tensor_tensor", out=a_in[:, kb * 4:(kb + 1) * 4, q0:q0 + nq], in0=stg[:, 0:W].rearrange("p (r q) -> p r q", r=4),
                                    in1=g_ag[:, kb * 4:(kb + 1) * 4, q0:q0 + nq], op=ALU.mult),
                          reads=[res(sres), res("g_ag")], writes=[res("a_in")])

                pk0 = scores_kb(0)
                yield
                pv(pk0, BO, BD)
                yield
                pk1 = scores_kb(1)
                yield
                norm(0, BO, BD)
                bo1, bd1 = next_bank(), next_bank()
                pv(pk1, bo1, bd1)
                norm(1, bo1, bd1)
                yield

        def pool_pre(t):
            n = 128 if t < NT else 32
            for g in range(4):
                w = 2 << g
                b0 = 2 * g
                mres = res("mixed%d" % g)
                if t < NT:
                    LL = 16 + n
                    units = [(lambda lo, hi, b0=b0: pu_sb[:, b0:b0 + 2, lo:hi],
                              lambda lo, hi: sA[:, :, lo:hi],
                              lambda lo, hi: sB[:, :, lo:hi],
                              lambda lo, hi, b0=b0: mixed[:, b0:b0 + 2, lo:hi])]
                    ures = res("pu")
                else:
                    LL = 32
                    units = []
                    for bb in range(2):
                        units.append((lambda lo, hi, bb=bb, b0=b0: pus[:, b0 + bb, :, lo:hi],
                                      lambda lo, hi, bb=bb: sA[:, bb, 0:64].rearrange("p (s c) -> p s c", s=2)[:, :, lo:hi],
                                      lambda lo, hi, bb=bb: sB[:, bb, 0:64].rearrange("p (s c) -> p s c", s=2)[:, :, lo:hi],
                                      lambda lo, hi, bb=bb, b0=b0: mixed[:, b0 + bb, 0:32].rearrange("p (s c) -> p s c", s=2)[:, :, lo:hi]))
                    ures = res("pus")
                for (U, A, B, MX) in units:
                    sc.op("pool", L("tensor_tensor", out=A(2, LL), in0=U(2, LL), in1=U(1, LL - 1), op=ALU.add), reads=[ures], writes=[res("sA")], extra=[c_all, WEV.get("c3")])
                    fin, fres = A, res("sA")
                    if w >= 4:
                        sc.op("pool", L("tensor_tensor", out=B(4, LL), in0=A(4, LL), in1=A(2, LL - 2), op=ALU.add), reads=[res("sA")], writes=[res("sB")])
                        fin, fres = B, res("sB")
                    if w >= 8:
                        sc.op("pool", L("tensor_tensor", out=A(8, LL), in0=B(8, LL), in1=B(4, LL - 4), op=ALU.add), reads=[res("sB")], writes=[res("sA")])
                        fin, fres = A, res("sA")
                    if w >= 16:
                        sc.op("pool", L("tensor_tensor", out=B(16, LL), in0=A(16, LL), in1=A(8, LL - 8), op=ALU.add), reads=[res("sA")], writes=[res("sB")])
                        fin, fres = B, res("sB")
                    sc.op("dve", L("scalar_tensor_tensor", out=MX(0, LL - 16), in0=fin(16, LL), scalar=1.0 / w, in1=U(16, LL), op0=ALU.mult, op1=ALU.subtract),
                          reads=[fres, ures], writes=[mres])
                    if t == 0:
                        sc.op("pool", L("tensor_tensor", out=fin(16, 32), in0=fin(16, 32), in1=invc[:, b0:b0 + 2, :], op=ALU.mult), reads=[fres, mres, res("mT")], writes=[fres], extra=[c_all])
                        sc.op("dve", L("tensor_tensor", out=MX(0, 16), in0=fin(16, 32), in1=U(16, 32), op=ALU.subtract), reads=[fres, ures], writes=[mres])
            if t < NT - 1:
                sc.op("pool", L("tensor_copy", out=pu_sb[:, :, 1:16], in_=pu_sb[:, :, n + 1:n + 16]), reads=[res("pu")], writes=[res("pu")])

        def pool_out(t):
            if t == NT - 1:
                jobs = [(pp, lambda blk: pu_sb[:, blk, 129:144], res("pu"))]
            else:
                jobs = [(spo[bq], (lambda blk, bq=bq: pus[:, blk, bq, 17:32]), res("pus")) for bq in range(2)]
            for (dst, getter, rr) in jobs:
                for half, (stg, sres) in enumerate(((o1, "o1"), (dsb, "dsb"))):
                    b = next_bank()
                    for i in range(4):
                        blk = 4 * half + i
                        sc.op("pe", L("transpose", out=b.t[0:15, i * 128:(i + 1) * 128], in_=getter(blk), identity=identf[:, :]),
                              reads=[rr], excl=[b], extra=[c_all], signal=(i == 3))
                    sc.op("act", L("activation", out=stg[0:15, :], in_=b.t[0:15, :], func=AF.Copy), writes=[res(sres)], excl=[b])
                    sc.dma("sp", L("dma_start", out=dst[:, half * 512:(half + 1) * 512], in_=stg[0:15, :]), "o_pool_" + sres, reads=[res(sres)], is_output=True)
                    yield

        def pool_post(t):
            n = 128 if t < NT else 32
            for gp in range(2):
                bpo = next_bank()
                for gg in range(2):
                    g = 2 * gp + gg
                    b0 = 2 * g
                    for ob in range(2):
                        for kc in range(2):
                            sc.op("pe", L("matmul", bpo.t[:, (2 * gg + ob) * 128:(2 * gg + ob) * 128 + n], lhsT=pw_sb[:, g, kc, ob * 128:(ob + 1) * 128], rhs=mixed[:, b0 + kc, 0:n], start=(kc == 0), stop=(kc == 1)),
                                  reads=[res("mixed%d" % g)], excl=[bpo], extra=[WEV["w_pw"]], signal=(gg == 1 and ob == 1 and kc == 1))
                for i in range(4):
                    blk = 4 * gp + i
                    sc.op("dve", L("scalar_tensor_tensor", out=p_in[:, blk, 0:n], in0=bpo.t[:, i * 128:i * 128 + n], scalar=psc8[:, blk:blk + 1], in1=g_pg[:, blk, 0:n], op0=ALU.mult, op1=ALU.mult),
                          reads=[res("g_pg")], writes=[res("p_in")], excl=[bpo], extra=[c_all])
                yield

        def gate(t):
            n = 128 if t < NT else 32
            for jp in range(4):
                b = next_bank()
                for which, (wsb_, src_, rname, wk) in enumerate(((wa_sb, a_in, "a_in", "w_wa"), (wp_sb, p_in, "p_in", "w_wp"))):
                    for jj in range(2):
                        jb = 2 * jp + jj
                        sl_ = which * 2 + jj
                        for kc in range(8):
                            sc.op("pe", L("matmul", b.t[:, sl_ * 128:sl_ * 128 + n], lhsT=wsb_[:, kc, jb * 128:(jb + 1) * 128], rhs=src_[:, kc, 0:n], start=(kc == 0), stop=(kc == 7)),
                                  reads=[res(rname)], excl=[b], extra=[WEV[wk]], signal=(which == 1 and jj == 1 and kc == 7))
                for jj in range(2):
                    jb = 2 * jp + jj
                    tj = t_ctr[0] % 2
                    t_ctr[0] += 1
                    sc.op("dve", L("scalar_tensor_tensor", out=t1[:, tj, 0:n], in0=th_ma[:, jb, 0:n], scalar=1.0, in1=b.t[:, jj * 128:jj * 128 + n], op0=ALU.add, op1=ALU.mult),
                          reads=[res("th_ma")], writes=[res("t1_%d" % tj)], excl=[b])
                    sc.op("dve", L("scalar_tensor_tensor", out=t2[:, tj, 0:n], in0=th_mp[:, jb, 0:n], scalar=1.0, in1=b.t[:, (2 + jj) * 128:(2 + jj) * 128 + n], op0=ALU.add, op1=ALU.mult),
                          reads=[res("th_mp")], writes=[res("t2_%d" % tj)], excl=[b])
                    sc.op("pool", L("tensor_tensor", out=mT[:, jb, 0:n], in0=t1[:, tj, 0:n], in1=t2[:, tj, 0:n], op=ALU.add),
                          reads=[res("t1_%d" % tj), res("t2_%d" % tj)], writes=[res("mT")])
                yield

        def final(t):
            n = 128 if t < NT else 32
            sl = t % 2
            for nb_, bk in enumerate((BO, BD)):
                for kc in range(8):
                    sc.op("pe", L("matmul", bk.t[0:n, :], lhsT=mT[:, kc, 0:n], rhs=wo_sb[:, kc, nb_ * 512:(nb_ + 1) * 512], start=(kc == 0), stop=(kc == 7)),
                          reads=[res("mT")], excl=[bk], extra=[WEV["w_wo"]], signal=(kc == 7))
                sc.op("dve", L("scalar_tensor_tensor", out=xbuf[0:n, sl, nb_ * 512:(nb_ + 1) * 512], in0=bk.t[0:n, :], scalar=0.5, in1=xbuf[0:n, sl, nb_ * 512:(nb_ + 1) * 512], op0=ALU.mult, op1=ALU.add),
                      reads=[res("xb%d" % sl)], writes=[res("xb%d" % sl)], excl=[bk])
                yield
            dst_ = yp[t * 128:(t + 1) * 128, :] if t < NT else ys
            sc.dma("sp", L("dma_start", out=dst_, in_=xbuf[0:n, sl, :]), "y%d" % sl, reads=[res("xb%d" % sl)], is_output=True)

        def once(fn, *a):
            fn(*a)
            yield

        def interleave(a, b, na=1, nb=1, until_a=False):
            da = db = False
            while not ((da and db) or (until_a and da)):
                for _ in range(na):
                    if not da:
                        try:
                            next(a)
                        except StopIteration:
                            da = True
                for _ in range(nb):
                    if not db:
                        try:
                            next(b)
                        except StopIteration:
                            db = True

        def sample_state():
            sc.dma("pool", L("dma_start", out=stk_tm[:, :, :], in_=sk.rearrange("b t c -> t b c")), "st_k", writes=[res("mixed0"), res("mixed1")])
            sc.dma("pool", L("dma_start", out=vst[:, :, :], in_=sv.rearrange("b t c -> t b c")), "st_v", writes=[res("pu")])
            for bq in range(2):
                for kb in range(2):
                    sc.op("pe", L("transpose", out=psT[:, bq * 2 + kb, :], in_=stk_tm[:, bq, kb * 128:(kb + 1) * 128], identity=ident_bf[:, :]),
                          extra=[WEV["c2"]], reads=[res("mixed0"), res("mixed1")], excl=[BT], signal=(bq == 1 and kb == 1))
            sc.op("act", L("activation", out=kTst[:, :, :, :].rearrange("p a b c -> p (a b c)"), in_=psT[:, 0:4, :].rearrange("p a c -> p (a c)"), func=AF.Copy),
                  writes=[res("pu")], excl=[BT])

        def front1(t):
            last = (t >= NT - 1)
            if t == NT:
                sample_state()
            if dbg.get("pipeqk", 0):
                xprep_b(t)
                sk_ = qk_a(t, "k", [0, 1])
                yield
                sq0 = qk_a(t, "q", [0, 1, 2, 3])
                yield
                qk_b(t, "k", [0, 1], sk_)
                sq1 = qk_a(t, "q", [4, 5, 6, 7])
                yield
                qk_b(t, "q", [0, 1, 2, 3], sq0)
                v_group(t)
                yield
                qk_b(t, "q", [4, 5, 6, 7], sq1)
            else:
                xprep_b(t)
                yield
                sk_ = qk_a(t, "k", [0, 1])
                v_group(t)
                qk_b(t, "k", [0, 1], sk_)
                yield
                sq0 = qk_a(t, "q", [0, 1, 2, 3])
                for _ in stage_blocks(t, "pu", only=0):
                    pass
                qk_b(t, "q", [0, 1, 2, 3], sq0)
                yield
                sq1 = qk_a(t, "q", [4, 5, 6, 7])
                for _ in stage_blocks(t, "pu", only=1):
                    pass
                qk_b(t, "q", [4, 5, 6, 7], sq1)
                yield
            if last:
                k_out(t)
            if dbg.get("pipeqk", 0):
                yield from stage_blocks(t, "pu")
            pool_pre(t)
            if last:
                yield from pool_out(t)
            yield from stage_blocks(t, "ag")
            yield from stage_blocks(t, "pg")

        def front2(t):
            yield from stage_blocks(t, "ma")
            yield from pool_post(t)
            yield from stage_blocks(t, "mp")

        def back2(t):
            yield from gate(t)
            yield from final(t)

        def late_setup(i):
            if i < 8:
                for bq in range(2):
                    sc.dma("sp", L("dma_start", out=pus[:, i, bq, 1:16], in_=spl[bq, :, i * 128:(i + 1) * 128].rearrange("t p -> p t"), allow_slow_non_contiguous=True), "c3")
            elif i == 8:
                for bq in range(2):
                    sc.dma("sp", L("dma_start", out=sko[bq, 0:112, :], in_=sk[bq, 16:128, :]), "o_st", is_output=True)
                    sc.dma("sp", L("dma_start", out=svo[bq, 0:112, :], in_=sv[bq, 16:128, :]), "o_st", is_output=True)

        tiles = dbg.get("tiles", list(range(NT + 1)))
        if tiles:
            xprep_a(tiles[0])
            load_perm(C_Q, "q")
            load_perm(C_AG, "ag")
            load_perm(C_MA, "ma", perm=False)
            for _ in front1(tiles[0]):
                pass
            bias_finish()
        for ti, t in enumerate(tiles):
            nxt = tiles[ti + 1] if ti + 1 < len(tiles) else None
            if nxt is not None:
                xload(nxt)
            if ti < 9:
                late_setup(ti)
            if "c3" not in WEV and "c3" in sc.dma_keys and (ti >= 8 or nxt is None or nxt == NT):
                WEV["c3"] = sc.group_total("c3")
            f2 = front2(t)
            att = attention(t)
            done = False
            if dbg.get("oldatt"):
                interleave(att, f2, 2, 1, until_a=True)
                done = True
            while not done:
                for step in ("a", "f", "f", "a"):
                    try:
                        next(att if step == "a" else f2)
                    except StopIteration:
                        if step == "a":
                            done = True
                            break
            if nxt is not None:
                xprep_a(nxt)
            for _ in f2:
                pass
            if nxt is not None:
                b2 = back2(t)
                interleave(b2, front1(nxt), 1, 1)
            else:
                for _ in back2(t):
                    pass

        if dbg.get("dump"):
            for nm, buf, shape, dt_ in (("qT", qT, [128, 8 * 128], BF16), ("kTr", kTr, [128, 4 * 128], BF16), ("vr", vr, [128, 512], BF16),
                                        ("g_ag", g_ag, [128, 1024], BF16), ("g_pg", g_pg, [128, 1024], BF16), ("a_in", a_in, [128, 1024], BF16),
                                        ("p_in", p_in, [128, 1024], BF16), ("mT", mT, [128, 1024], BF16), ("th_ma", th_ma, [128, 1024], BF16),
                                        ("th_mp", th_mp, [128, 1024], BF16), ("pu_sb", pu_sb, [128, 8 * 144], F32), ("hT", hT, [128, 1024], BF16),
                                        ("E", E, [128, 2 * 16 * 128], BF16)):
                dtn = nc.dram_tensor("dbg_" + nm, shape, dt_, kind="ExternalOutput").ap()
                flat = buf
                nd = len(buf.shape)
                if nd == 3:
                    flat = buf[:, :, :].rearrange("p a b -> p (a b)")
                elif nd == 4:
                    flat = buf[:, :, :, :].rearrange("p a b c -> p (a b c)")
                sc.dma("sp", L("dma_start", out=dtn, in_=flat), "dbg", reads=[res(k) for k in list(R.keys())])
        if dbg.get("waitw"):
            for k in dbg["waitw"]:
                sc.op("act", L("activation", out=st[:, 0, 3:4], in_=st[:, 0, 3:4], func=AF.Copy), extra=[WEV[k]])
        sc.emit(block)
    return nc


def _bucket_onehot():
    rel = (127 - np.arange(384)).astype(np.int32)
    nb = 16
    ret = np.where(rel > 0, nb, 0)
    n = np.abs(rel)
    max_exact = nb // 2
    ratio = np.maximum(n, 1).astype(np.float32) / np.float32(max_exact)
    large = max_exact + (np.log(ratio).astype(np.float32) / np.float32(math.log(128 / max_exact))
                         * np.float32(nb - max_exact)).astype(np.int32)
    large = np.minimum(large, nb - 1)
    bucket = ret + np.where(n < max_exact, n, large)
    G = np.zeros((32, 384), np.float32)
    G[bucket, np.arange(384)] = 1.0
    return G


_CACHE = {}


def kernel(x_prompt, x_sample, state_attn_k, state_attn_v, state_pool, norm_gain, w_in,
           q_norm_gain, k_norm_gain, attn_sinks, rel_bias, pool_w, pool_scale,
           w_attn_br, w_pool_br, w_out):
    f = lambda a: np.ascontiguousarray(np.asarray(a, dtype=np.float32))
    if "nc" not in _CACHE:
        _CACHE["nc"] = build_program()
        _CACHE["G"] = _bucket_onehot()
    nc = _CACHE["nc"]
    bd = np.zeros((128, 128), np.float32)
    bd[0:64, 0:64] = 1.0 / 64.0
    bd[64:128, 64:128] = 1.0 / 64.0
    inv = np.zeros((128, 8, 16), np.float32)
    for blk in range(8):
        w = 2 << (blk // 2)
        inv[:, blk, :] = 1.0 / np.minimum(w, np.arange(16) + 1)
    common = {
        "ng": f(norm_gain), "win": f(w_in[0]), "qg": f(q_norm_gain), "kg": f(k_norm_gain),
        "sinks": f(attn_sinks), "rb": f(rel_bias), "pw": f(pool_w[0]), "psc": f(pool_scale),
        "wa": f(w_attn_br[0]), "wp": f(w_pool_br[0]), "wo": f(w_out[0]),
        "cG": _CACHE["G"], "cI": np.eye(128, dtype=np.float32), "cBD": bd, "cINV": inv.reshape(128, 128),
    }
    x_prompt = f(x_prompt)
    x_sample = f(x_sample)
    sk_ = f(state_attn_k)[0].reshape(16, 128, 256)
    sv_ = f(state_attn_v)[0].reshape(16, 128, 256)
    spl_ = f(state_pool)[0]
    in_maps = []
    for c in range(NCORES):
        m = dict(common)
        m["xp"] = x_prompt[c]
        m["xs"] = np.ascontiguousarray(x_sample[2 * c:2 * c + 2].reshape(32, D))
        m["sk"] = np.ascontiguousarray(sk_[2 * c:2 * c + 2])
        m["sv"] = np.ascontiguousarray(sv_[2 * c:2 * c + 2])
        m["spl"] = np.ascontiguousarray(spl_[2 * c:2 * c + 2])
        in_maps.append(m)
    res = run_bass_kernel_spmd(nc, in_maps, core_ids=list(range(NCORES)))
    r = res.results
    y_p = np.stack([r[c]["yp"] for c in range(NCORES)], 0)
    y_s = np.concatenate([r[c]["ys"].reshape(2, 16, D) for c in range(NCORES)], 0)
    p_k = np.stack([r[c]["pk"].reshape(128, 4, 64) for c in range(NCORES)], 0)[None]
    p_v = np.stack([r[c]["pv"].reshape(128, 4, 64) for c in range(NCORES)], 0)[None]
    p_p = np.stack([r[c]["pp"] for c in range(NCORES)], 0)[None]
    s_k = np.concatenate([r[c]["sko"].reshape(2, 128, 4, 64) for c in range(NCORES)], 0)[None]
    s_v = np.concatenate([r[c]["svo"].reshape(2, 128, 4, 64) for c in range(NCORES)], 0)[None]
    s_p = np.concatenate([r[c]["spo"] for c in range(NCORES)], 0)[None]
    return (y_p.astype(np.float32), y_s.astype(np.float32), p_k.astype(np.float32), p_v.astype(np.float32),
            p_p.astype(np.float32), s_k.astype(np.float32), s_v.astype(np.float32), s_p.astype(np.float32))
```

```python
from contextlib import ExitStack
import math
import numpy as np
import concourse.bass as bass
import concourse.mybir as mybir
from concourse.bass_utils import run_bass_kernel_spmd

F32 = mybir.dt.float32
BF16 = mybir.dt.bfloat16
ALU = mybir.AluOpType
AF = mybir.ActivationFunctionType

NCORES = 8
SEQ = 2048
D = 1024
NT = SEQ // 128
EPS = 1e-6
C_Q, C_K, C_V, C_AG, C_PU, C_PG, C_MA, C_MP = 0, 1024, 1280, 1536, 2560, 3584, 4608, 5632


class Ev:
    __slots__ = ("sem", "value")

    def __init__(self, sem, value):
        self.sem = sem
        self.value = value


class Res:
    __slots__ = ("name", "w", "r")

    def __init__(self, name):
        self.name = name
        self.w = None
        self.r = []


class Bank:
    def __init__(self, name, t):
        self.name = name
        self.t = t
        self.last = {}


class Stream:
    def __init__(self, name, sem):
        self.name = name
        self.sem = sem
        self.count = 0
        self.ops = []
        self.waited = {}


class Sched:
    def __init__(self, nc, es):
        self.nc = nc
        self.es = es
        self.streams = {}
        for n in ("pe", "act", "dve", "pool", "sp"):
            sem = es.enter_context(nc.semaphore("s_" + n))
            self.streams[n] = Stream(n, sem)
        self.dma_keys = {}
        self.out_events = []

    def _deps(self, reads, writes, extra):
        deps = [e for e in extra if e is not None]
        for r in reads:
            if r.w is not None:
                deps.append(r.w)
        for w in writes:
            if w.w is not None:
                deps.append(w.w)
            deps.extend(w.r)
        return deps

    def _waits(self, st, deps):
        best = {}
        for d in deps:
            k = id(d.sem)
            if k not in best or best[k].value < d.value:
                best[k] = d
        waits = []
        for k, d in best.items():
            if st.waited.get(k, -1) >= d.value:
                continue
            st.waited[k] = d.value
            waits.append(d)
        return waits

    def _commit(self, ev, reads, writes):
        for r in reads:
            r.r.append(ev)
        for w in writes:
            w.w = ev
            w.r = []

    def op(self, stream, fn, reads=(), writes=(), extra=(), signal=True, excl=()):
        st = self.streams[stream]
        deps = self._deps(reads, writes, extra)
        for b in excl:
            deps.extend(ev for s, ev in b.last.items() if s != stream)
        if stream == "pe":
            deps = [d for d in deps if d.sem is not st.sem]
        waits = self._waits(st, deps)
        if not signal:
            st.ops.append((waits, fn, None))
            return None
        st.count += 1
        ev = Ev(st.sem, st.count)
        st.ops.append((waits, fn, (st.sem, 1)))
        self._commit(ev, reads, writes)
        for b in excl:
            b.last[stream] = ev
        return ev

    def dma(self, stream, fn, key, reads=(), writes=(), extra=(), is_output=False):
        st = self.streams[stream]
        if key not in self.dma_keys:
            sem = self.es.enter_context(self.nc.semaphore("d_" + key))
            self.dma_keys[key] = [sem, 0]
        ent = self.dma_keys[key]
        deps = [d for d in self._deps(reads, writes, extra) if d.sem is not ent[0]]
        waits = self._waits(st, deps)
        ent[1] += 16
        ev = Ev(ent[0], ent[1])
        st.ops.append((waits, fn, (ent[0], 16)))
        self._commit(ev, reads, writes)
        if is_output:
            self.out_events.append(ev)
        return ev

    def group_total(self, key):
        ent = self.dma_keys[key]
        return Ev(ent[0], ent[1])

    def emit(self, block):
        sch = self

        def run(st, eng):
            for waits, fn, inc in st.ops:
                for w in waits:
                    eng.wait_ge(w.sem, w.value)
                ins = fn(eng)
                if inc is not None:
                    ins.then_inc(inc[0], inc[1])

        finals = [Ev(ent[0], ent[1]) for ent in self.dma_keys.values()]

        @block.sync
        def _(e):
            run(sch.streams["sp"], e)
            for ev in finals:
                e.wait_ge(ev.sem, ev.value)

        @block.scalar
        def _(e):
            run(sch.streams["act"], e)

        @block.vector
        def _(e):
            run(sch.streams["dve"], e)

        @block.gpsimd
        def _(e):
            run(sch.streams["pool"], e)

        @block.tensor
        def _(e):
            run(sch.streams["pe"], e)


def L(f, *args, **kw):
    return lambda e: getattr(e, f)(*args, **kw)


def build_program(dbg=None):
    dbg = dbg or {}
    nc = bass.Bass("TRN2", target_bir_lowering=False, dynamic_dma_scratch_size=2048)

    def din(name, shape):
        return nc.dram_tensor(name, shape, F32, kind="ExternalInput").ap()

    def dout(name, shape):
        return nc.dram_tensor(name, shape, F32, kind="ExternalOutput").ap()

    xp = din("xp", [SEQ, D])
    xs = din("xs", [32, D])
    sk = din("sk", [2, 128, 256])
    sv = din("sv", [2, 128, 256])
    spl = din("spl", [2, 15, D])
    ng = din("ng", [1, D])
    win = din("win", [D, 6656])
    qg = din("qg", [1, 64])
    kg = din("kg", [1, 64])
    sinks = din("sinks", [1, 16])
    rb = din("rb", [32, 16])
    pw = din("pw", [4, 256, 256])
    psc = din("psc", [1, D])
    wa = din("wa", [D, D])
    wp = din("wp", [D, D])
    wo = din("wo", [D, D])
    cG = din("cG", [32, 384])
    cI = din("cI", [128, 128])
    cBD = din("cBD", [128, 128])
    cINV = din("cINV", [128, 8 * 16])

    yp = dout("yp", [SEQ, D])
    ys = dout("ys", [32, D])
    pk = dout("pk", [128, 256])
    pv = dout("pv", [128, 256])
    pp = dout("pp", [15, D])
    sko = dout("sko", [2, 128, 256])
    svo = dout("svo", [2, 128, 256])
    spo = dout("spo", [2, 15, D])
    scr = nc.dram_tensor("scr", [16, 128 * 384], F32, kind="Internal").ap()

    with ExitStack() as es:
        def sb(name, shape, dt):
            return es.enter_context(nc.sbuf_tensor(name, shape, dt))

        def ps(name, shape, dt):
            return es.enter_context(nc.psum_tensor(name, shape, dt))

        win_sb = sb("win_sb", [128, 8, 6656], BF16)
        wa_sb = sb("wa_sb", [128, 8, D], BF16)
        wp_sb = sb("wp_sb", [128, 8, D], BF16)
        wo_sb = sb("wo_sb", [128, 8, D], BF16)
        pw_sb = sb("pw_sb", [128, 4, 2, 256], BF16)
        xbuf = sb("xbuf", [128, 2, D], F32)
        hp = sb("hp", [128, D], BF16)
        hT = sb("hT", [128, 8, 128], BF16)
        gain_bc = sb("gain_bc", [128, D], BF16)
        E = sb("E", [128, 2, 16, 128], BF16)
        qT = sb("qT", [128, 8, 128], BF16)
        kTr = sb("kTr", [128, 2, 2, 128], BF16)
        vr = sb("vr", [128, 2, 256], BF16)
        ptraw = sb("ptraw", [128, 2, 512], BF16)
        pt = sb("pt", [128, 4, 512], BF16)
        dsb = sb("dsb", [128, 512], F32)
        o1 = sb("o1", [128, 512], F32)
        g_ag = sb("g_ag", [128, 8, 128], BF16)
        g_pg = sb("g_pg", [128, 8, 128], BF16)
        a_in = sb("a_in", [128, 8, 128], BF16)
        p_in = sb("p_in", [128, 8, 128], BF16)
        mT = sb("mT", [128, 8, 128], BF16)
        th_ma = sb("th_ma", [128, 8, 128], BF16)
        th_mp = sb("th_mp", [128, 8, 128], BF16)
        pu_sb = sb("pu_sb", [128, 8, 144], F32)
        sA = sb("sA", [128, 2, 144], F32)
        sB = sb("sB", [128, 2, 144], F32)
        mixed = sb("mixed", [128, 8, 128], BF16)
        t1 = sb("t1", [128, 2, 128], F32)
        t2 = sb("t2", [128, 2, 128], F32)
        pus = sb("pus", [128, 8, 2, 32], F32)
        ident_bf = sb("ident_bf", [128, 128], BF16)
        identf = sb("identf", [128, 128], F32)
        bd_bf = sb("bd_bf", [128, 128], BF16)
        ones_bf = sb("ones_bf", [128, 64], BF16)
        qgcol = sb("qgcol", [128, 1], F32)
        kgcol = sb("kgcol", [128, 1], F32)
        sink_t = sb("sink_t", [128, 8], F32)
        esink = sb("esink", [128, 8], F32)
        psc8 = sb("psc8", [128, 8], F32)
        st = sb("st", [128, 2, 4], F32)
        kf32 = o1[:, 0:256].rearrange("p (a b) -> p a b", a=2)
        kout = o1[:, 256:512]
        sq = ptraw
        rs = dsb
        G_sb = o1[0:32, 0:384]
        pu_bf = pu_sb[:, :, :].rearrange("p a b -> p (a b)").bitcast(BF16)
        kTst = pu_bf[:, 0:512].rearrange("p (a b c) -> p a b c", a=2, b=2)
        vst = pu_bf[:, 512:1024].rearrange("p (a c) -> p a c", a=2)
        stk_tm = mixed[:, 0:4, :].rearrange("p (a b) c -> p a (b c)", a=2)
        invc = mT[:, 0:2, :].bitcast(F32).rearrange("p a (b c) -> p (a b) c", c=16)
        tb = o1[0:32, 384:400]
        S_sb = dsb[0:16, 0:384]

        psT = ps("psT", [128, 8, 128], BF16)
        pbt = [ps("pb%d" % i, [128, 512], F32) for i in range(7)]
        BT = Bank("psT", psT)
        BB = [Bank("bb%d" % i, pbt[i]) for i in range(3)]
        BS = [Bank("bs%d" % i, pbt[3 + i]) for i in range(2)]
        BO = Bank("bo", pbt[5])
        BD = Bank("bd", pbt[6])
        ps_o, ps_d = BO.t, BD.t

        sc = Sched(nc, es)
        R = {}

        def res(n):
            if n not in R:
                R[n] = Res(n)
            return R[n]

        block = es.enter_context(nc.Block())

        def cdma(out, in_, wr, slow=False):
            if slow:
                return sc.dma("sp", L("dma_start", out=out, in_=in_, allow_slow_non_contiguous=True), "c", writes=[res(wr)])
            return sc.dma("sp", L("dma_start", out=out, in_=in_), "c", writes=[res(wr)])

        _t0 = dbg.get("tiles", [0])
        if _t0:
            _n0 = 128 if _t0[0] < NT else 32
            _src0 = xp[_t0[0] * 128:(_t0[0] + 1) * 128, :] if _t0[0] < NT else xs
            sc.dma("sp", L("dma_start", out=xbuf[0:_n0, _t0[0] % 2, :], in_=_src0), "x%d" % (_t0[0] % 2), writes=[res("xb%d" % (_t0[0] % 2))])
        cdma(tb, rb, "o1")
        cdma(G_sb, cG, "o1")
        cdma(identf[:, :], cI, "identf")
        for hf in range(2):
            cdma(qgcol[hf * 64:(hf + 1) * 64, :], qg.rearrange("o d -> d o"), "qgcol%d" % hf)
            cdma(kgcol[hf * 64:(hf + 1) * 64, :], kg.rearrange("o d -> d o"), "kgcol%d" % hf)
            cdma(sink_t[hf * 64:(hf + 1) * 64, :].rearrange("p (a b) -> p a b", a=2),
                 bass.AP(tensor=sinks.tensor, offset=4 * hf, ap=[[0, 64], [8, 2], [1, 4]]), "sink%d" % hf)
        cdma(psc8[:, :], psc.rearrange("o (b p) -> p (o b)", p=128), "psc8", slow=True)
        cdma(invc, cINV.rearrange("p (a b) -> p a b", a=8), "mT")
        R["qgcol"] = res("qgcol1")
        R["kgcol"] = res("kgcol1")
        c_all = None

        sc.op("pool", L("memset", ones_bf[:, :], 1.0), writes=[res("ones")])
        sc.op("pool", L("memset", pu_sb[:, :, 0:16], 0.0), writes=[res("pu")])
        sc.op("pool", L("memset", st[:, :, :], 0.0), writes=[res("st0"), res("st1")])
        def wdma(out, in_, key):
            sc.dma("pool", L("dma_start", out=out, in_=in_), key)

        wdma(ident_bf[:, :], cI, "c2")
        wdma(bd_bf[:, :], cBD, "c2")
        wdma(gain_bc[:, :], bass.AP(tensor=ng.tensor, offset=0, ap=[[0, 128], [1, D]]), "c2")

        def w_nat(wsb, wsrc, c0, C, key):
            wdma(wsb[:, :, c0:c0 + C], wsrc[:, c0:c0 + C].rearrange("(kc p) c -> p kc c", p=128), key)

        w_nat(win_sb, win, C_K, 512, "w_kv")
        for i in range(2):
            w_nat(win_sb, win, C_PU + 512 * i, 512, "w_pu")
        for i in range(2):
            w_nat(win_sb, win, C_PG + 512 * i, 512, "w_pg")
        for i in range(2):
            w_nat(win_sb, win, C_MP + 512 * i, 512, "w_mp")
        wdma(pw_sb[:, :, :, :], pw.rearrange("g (kc p) c -> p g kc c", p=128), "w_pw")
        for kb in range(2):
            for hf in range(2):
                wdma(wa_sb[hf * 64:(hf + 1) * 64, kb * 4:(kb + 1) * 4, :],
                     wa[512 * kb + 256 * hf:512 * kb + 256 * hf + 256, :].rearrange("(r d) c -> d r c", d=64), "w_wa")
        for i in range(2):
            w_nat(wp_sb, wp, 512 * i, 512, "w_wp")
        for i in range(2):
            w_nat(wo_sb, wo, 512 * i, 512, "w_wo")
        WEV = {k: sc.group_total(k) for k in ["c2", "w_kv", "w_pu", "w_pg", "w_pw", "w_mp", "w_wa", "w_wp", "w_wo"]}
        WEV["w_v"] = WEV["w_kv"]
        c_all = sc.group_total("c")

        sc.op("act", L("activation", out=esink[:, :], in_=sink_t[:, :], func=AF.Exp), extra=[c_all], writes=[res("esink")])

        sc.op("pe", L("matmul", BB[0].t[0:16, 0:384], lhsT=tb, rhs=G_sb, start=True, stop=True), extra=[c_all], reads=[res("o1")], excl=[BB[0]])
        sc.op("act", L("activation", out=S_sb, in_=BB[0].t[0:16, 0:384], func=AF.Copy), writes=[res("dsb")], excl=[BB[0]])
        sc.dma("sp", L("dma_start", out=scr.rearrange("h (r i) -> h r i", i=384), in_=S_sb.unsqueeze(1).to_broadcast([16, 128, 384])), "scrw", reads=[res("dsb")], writes=[res("scr")])
        def bias_finish():
            for ab, c0 in ((0, 255), (1, 127)):
                for hh in range(2):
                    sc.dma("sp", L("dma_start", out=xbuf[:, 1, :].rearrange("p (h q) -> p h q", h=8),
                                   in_=bass.AP(tensor=scr.tensor, offset=c0 + hh * 8 * 128 * 384, ap=[[383, 128], [128 * 384, 8], [1, 128]])),
                           "toe", reads=[res("scr")], writes=[res("xb1")])
                    sc.op("act", L("activation", out=E[:, ab, hh * 8:(hh + 1) * 8, :].rearrange("p h q -> p (h q)"), in_=xbuf[:, 1, :], func=AF.Exp), reads=[res("xb1")], writes=[res("E")])
            sc.op("dve", L("memset", E[0:64, 0, :, 64:128], 0.0), writes=[res("E")])
            sc.op("dve", L("memset", E[64:128, 1, :, 0:64], 0.0), writes=[res("E")])


        stage_bufs = [(th_ma, "th_ma"), (th_mp, "th_mp"), (p_in, "p_in"), (g_pg, "g_pg")]
        stage_ctr = [0]
        WQ = {}

        def load_perm(c0, name, perm=True):
            rounds = [(kb, kc) for kb in range(2) for kc in range(8)]
            slots = []

            def issue(i):
                kb, kc = rounds[i]
                buf, rn = stage_bufs[stage_ctr[0] % 4]
                stage_ctr[0] += 1
                stg = buf[:, :, :].rearrange("p a b -> p (a b)").bitcast(F32)
                sc.dma("act", L("dma_start", out=stg, in_=win[kc * 128:(kc + 1) * 128, c0 + kb * 512:c0 + (kb + 1) * 512]), "wst_" + rn, writes=[res(rn)])
                slots.append((stg, rn))

            for i in range(4):
                issue(i)
            ev = None
            for i, (kb, kc) in enumerate(rounds):
                stg, rn = slots[i]
                if perm:
                    ev = sc.op("act", L("activation", out=win_sb[:, kc, c0 + kb * 512:c0 + (kb + 1) * 512].rearrange("p (r hf d) -> p r hf d", r=4, hf=2),
                                        in_=stg.rearrange("p (hf r d) -> p r hf d", hf=2, r=4), func=AF.Copy), reads=[res(rn)])
                else:
                    ev = sc.op("act", L("activation", out=win_sb[:, kc, c0 + kb * 512:c0 + (kb + 1) * 512], in_=stg, func=AF.Copy), reads=[res(rn)])
                if i + 4 < len(rounds):
                    issue(i + 4)
            WQ[name] = ev


        bb_ctr = [0]

        ROT = BB + BS

        held = set()

        def next_bank():
            while True:
                b = ROT[bb_ctr[0] % len(ROT)]
                bb_ctr[0] += 1
                if b.name not in held:
                    return b

        pt_ctr = [0]
        sq_ctr = [0]
        s_ctr = [0]
        t_ctr = [0]

        def V(ap2d, nb, n):
            return ap2d[:, 0:nb * 128].rearrange("p (b c) -> p b c", c=128)[:, :, 0:n]

        xloaded = set(dbg.get("tiles", [0])[:1])

        def xload(t):
            sl = t % 2
            n = 128 if t < NT else 32
            src_ = xp[t * 128:(t + 1) * 128, :] if t < NT else xs
            sc.dma("sp", L("dma_start", out=xbuf[0:n, sl, :], in_=src_), "x%d" % sl, writes=[res("xb%d" % sl)])
            xloaded.add(t)

        def xprep_a(t):
            sl = t % 2
            n = 128 if t < NT else 32
            src_ = xp[t * 128:(t + 1) * 128, :] if t < NT else xs
            if not (dbg.get("noprefetch") is None and t in xloaded):
                sc.dma("sp", L("dma_start", out=xbuf[0:n, sl, :], in_=src_), "x%d" % sl, writes=[res("xb%d" % sl)])
            sc.op("act", L("activation", out=hp[0:n, :], in_=xbuf[0:n, sl, :], func=AF.Square, scale=1.0 / 32.0, accum_out=st[0:n, sl, 0:1]),
                  reads=[res("xb%d" % sl)], writes=[res("hp"), res("st%d" % sl)])
            sc.op("act", L("activation", out=st[0:n, sl, 1:2], in_=st[0:n, sl, 0:1], func=AF.Ln, bias=EPS), reads=[res("st%d" % sl)], writes=[res("st%d" % sl)])
            sc.op("act", L("activation", out=st[0:n, sl, 2:3], in_=st[0:n, sl, 1:2], func=AF.Exp, scale=-0.5), reads=[res("st%d" % sl)], writes=[res("st%d" % sl)])
            sc.op("dve", L("scalar_tensor_tensor", out=hp[0:n, :], in0=xbuf[0:n, sl, :], scalar=st[0:n, sl, 2:3], in1=gain_bc[0:n, :], op0=ALU.mult, op1=ALU.mult),
                  reads=[res("xb%d" % sl), res("st%d" % sl)], writes=[res("hp")], extra=[WEV["c2"]])
            sc.op("pool", L("memset", st[0:n, sl, 0:1], 0.0), reads=[], writes=[res("st%d" % sl)])

        def xprep_b(t):
            n = 128 if t < NT else 32
            for kc in range(8):
                sc.op("pe", L("transpose", out=psT[:, kc, 0:n], in_=hp[0:n, kc * 128:(kc + 1) * 128], identity=ident_bf[0:n, 0:n]),
                      reads=[res("hp")], excl=[BT], extra=[WEV["c2"]], signal=(kc == 7))
            sc.op("act", L("activation", out=hT[:, :, 0:n], in_=psT[:, :, 0:n], func=AF.Copy), writes=[res("hT")], excl=[BT])

        def mm_group(n, cols, wev):
            b = next_bank()
            nb = len(cols)
            for bi, col0 in enumerate(cols):
                for kc in range(8):
                    sc.op("pe", L("matmul", b.t[:, bi * 128:bi * 128 + n], lhsT=win_sb[:, kc, col0:col0 + 128], rhs=hT[:, kc, 0:n], start=(kc == 0), stop=(kc == 7)),
                          reads=[res("hT")], excl=[b], extra=[wev], signal=(bi == nb - 1 and kc == 7))
            return b

        def qk_a(t, typ, idxs):
            n = 128 if t < NT else 32
            nb = len(idxs)
            base = C_K if typ == "k" else C_Q
            b = mm_group(n, [base + 128 * i for i in idxs], WEV["w_kv"] if typ == "k" else WQ["q"])
            sqs = sq_ctr[0] % 2
            sq_ctr[0] += 1
            sc.op("act", L("activation", out=V(sq[:, sqs, :], nb, n), in_=V(b.t, nb, n), func=AF.Square), writes=[res("ptraw%d" % sqs)], excl=[b])
            held.add(b.name)
            return (b, sqs)

        def qk_b(t, typ, idxs, st_):
            b, sqs = st_
            n = 128 if t < NT else 32
            par = t % 2
            last = (t >= NT - 1)
            nb = len(idxs)
            b2 = next_bank()
            sqv = sq[:, sqs, :]
            if n == 128:
                sc.op("pe", L("matmul", V(b2.t, nb, n), lhsT=bd_bf[:, :], rhs=V(sqv, nb, n), start=True, stop=True), reads=[res("ptraw%d" % sqs)], excl=[b2], extra=[WEV["c2"]])
            else:
                for bi_ in range(nb):
                    sc.op("pe", L("matmul", b2.t[:, bi_ * 128:bi_ * 128 + n], lhsT=bd_bf[:, :], rhs=sqv[:, bi_ * 128:bi_ * 128 + n], start=True, stop=True),
                          reads=[res("ptraw%d" % sqs)], excl=[b2], extra=[WEV["c2"]], signal=(bi_ == nb - 1))
            sc.op("act", L("activation", out=V(rs, nb, n), in_=V(b2.t, nb, n), func=AF.Ln, bias=EPS), writes=[res("dsb")], excl=[b2])
            sc.op("act", L("activation", out=V(rs, nb, n), in_=V(rs, nb, n), func=AF.Exp, scale=-0.5), reads=[res("dsb")], writes=[res("dsb")])
            if typ == "k":
                dst, dres, gcol = kTr[:, :, par, 0:n], res("kT%d" % par), kgcol
            else:
                dst, dres, gcol = qT[:, idxs[0]:idxs[0] + nb, 0:n], res("qT"), qgcol
            sc.op("dve", L("scalar_tensor_tensor", out=dst, in0=V(b.t, nb, n), scalar=gcol[:, 0:1], in1=V(rs, nb, n), op0=ALU.mult, op1=ALU.mult),
                  reads=[res("dsb")], writes=[dres], excl=[b], extra=[c_all])
            if typ == "k" and last:
                sc.op("dve", L("scalar_tensor_tensor", out=kf32[:, :, 0:n], in0=V(b.t, nb, n), scalar=gcol[:, 0:1], in1=V(rs, nb, n), op0=ALU.mult, op1=ALU.mult),
                      reads=[res("dsb")], writes=[res("o1")], excl=[b])
            held.discard(b.name)

        def v_group(t):
            n = 128 if t < NT else 32
            par = t % 2
            last = (t >= NT - 1)
            if t < NT:
                b = next_bank()
                for kc in range(8):
                    sc.op("pe", L("matmul", b.t[0:n, 0:256], lhsT=hT[:, kc, 0:n], rhs=win_sb[:, kc, C_V:C_V + 256], start=(kc == 0), stop=(kc == 7)),
                          reads=[res("hT")], excl=[b], extra=[WEV["w_v"]], signal=(kc == 7))
                sc.op("act", L("activation", out=vr[0:n, par, :], in_=b.t[0:n, 0:256], func=AF.Copy), writes=[res("vr%d" % par)], excl=[b])
                if last:
                    vo = t1[:, :, :].rearrange("p a b -> p (a b)")
                    sc.op("act", L("activation", out=vo[0:n, :], in_=b.t[0:n, 0:256], func=AF.Copy), writes=[res("t1_0"), res("t1_1")], excl=[b])
                    sc.dma("sp", L("dma_start", out=pv, in_=vo[0:n, :]), "o_v0", reads=[res("t1_0"), res("t1_1")], is_output=True)
            else:
                for bq in range(2):
                    b = next_bank()
                    for kc in range(8):
                        sc.op("pe", L("matmul", b.t[0:16, 0:256], lhsT=hT[:, kc, bq * 16:(bq + 1) * 16], rhs=win_sb[:, kc, C_V:C_V + 256], start=(kc == 0), stop=(kc == 7)),
                              reads=[res("hT")], excl=[b], extra=[WEV["w_v"]], signal=(kc == 7))
                    sc.op("act", L("activation", out=vr[0:16, bq, :], in_=b.t[0:16, 0:256], func=AF.Copy), writes=[res("vr%d" % bq)], excl=[b])
                    tt = t1 if bq == 0 else t2
                    vo = tt[:, :, :].rearrange("p a b -> p (a b)")
                    tr = [res("t%d_0" % (bq + 1)), res("t%d_1" % (bq + 1))]
                    sc.op("act", L("activation", out=vo[0:16, :], in_=b.t[0:16, 0:256], func=AF.Copy), writes=tr, excl=[b])
                    sc.dma("sp", L("dma_start", out=svo[bq, 112:128, :], in_=vo[0:16, :]), "o_v%d" % bq, reads=tr, is_output=True)

        def k_out(t):
            n = 128 if t < NT else 32
            b = next_bank()
            for kb in range(2):
                sc.op("pe", L("transpose", out=b.t[0:n, kb * 128:(kb + 1) * 128], in_=kf32[:, kb, 0:n], identity=identf[:, :]),
                      reads=[res("o1")], excl=[b], extra=[c_all], signal=(kb == 1))
            sc.op("act", L("activation", out=kout[0:n, :], in_=b.t[0:n, 0:256], func=AF.Copy), writes=[res("o1")], excl=[b])
            if t < NT:
                sc.dma("sp", L("dma_start", out=pk, in_=kout[0:n, :]), "o_k", reads=[res("o1")], is_output=True)
            else:
                for bq in range(2):
                    sc.dma("sp", L("dma_start", out=sko[bq, 112:128, :], in_=kout[bq * 16:(bq + 1) * 16, :]), "o_k", reads=[res("o1")], is_output=True)

        def stage_blocks(t, typ, only=None):
            n = 128 if t < NT else 32
            for gi in (range(2) if only is None else [only]):
                idxs = [4 * gi + i for i in range(4)]
                if typ == "pu":
                    b = mm_group(n, [C_PU + 128 * i for i in idxs], WEV["w_pu"])
                    if t < NT:
                        sc.op("act", L("activation", out=pu_sb[:, 4 * gi:4 * gi + 4, 16:16 + n], in_=V(b.t, 4, n), func=AF.Copy), writes=[res("pu")], excl=[b])
                    else:
                        for i in range(4):
                            sc.op("act", L("activation", out=pus[:, 4 * gi + i, :, 16:32], in_=b.t[:, i * 128:i * 128 + 32].rearrange("p (b c) -> p b c", b=2), func=AF.Copy),
                                  writes=[res("pus")], excl=[b], extra=[c_all])
                elif typ in ("ag", "pg"):
                    c0, wev_, dst = (C_AG, WQ["ag"], g_ag) if typ == "ag" else (C_PG, WEV["w_pg"], g_pg)
                    b = mm_group(n, [c0 + 128 * i for i in idxs], wev_)
                    sc.op("act", L("activation", out=dst[:, 4 * gi:4 * gi + 4, 0:n], in_=V(b.t, 4, n), func=AF.Silu), writes=[res("g_" + typ)], excl=[b])
                else:
                    c0, wev_, dst = (C_MA, WQ["ma"], th_ma) if typ == "ma" else (C_MP, WEV["w_mp"], th_mp)
                    b = mm_group(n, [c0 + 128 * i for i in idxs], wev_)
                    sc.op("act", L("activation", out=dst[:, 4 * gi:4 * gi + 4, 0:n], in_=V(b.t, 4, n), func=AF.Tanh, scale=0.5), writes=[res("th_" + typ)], excl=[b])
                yield

        def attention(t):
            par = t % 2
            if t < NT:
                seqs = [(0, 128)]
            else:
                seqs = [(0, 16), (16, 16)]
            for bi, (q0, nq) in enumerate(seqs):
                if t < NT:
                    ktiles = []
                    if t > 0:
                        ktiles.append((lambda hf, kb: kTr[hf * 64:(hf + 1) * 64, kb, 1 - par, 0:128], lambda g: vr[0:128, 1 - par, g * 64:(g + 1) * 64], 128, 0,
                                       [res("kT%d" % (1 - par)), res("vr%d" % (1 - par))]))
                    ktiles.append((lambda hf, kb: kTr[hf * 64:(hf + 1) * 64, kb, par, 0:128], lambda g: vr[0:128, par, g * 64:(g + 1) * 64], 128, 1,
                                   [res("kT%d" % par), res("vr%d" % par)]))
                else:
                    ktiles = [(lambda hf, kb, bi=bi: kTst[hf * 64:(hf + 1) * 64, bi, kb, :], lambda g, bi=bi: vst[0:128, bi, g * 64:(g + 1) * 64], 128, 0, [res("pu")]),
                              (lambda hf, kb, q0=q0: kTr[hf * 64:(hf + 1) * 64, kb, par, q0:q0 + 16], lambda g, bi=bi: vr[0:16, bi, g * 64:(g + 1) * 64], 16, 1,
                               [res("kT%d" % par), res("vr%d" % bi)])]
                W = 4 * nq

                def scores_kb(kb):
                    ptl = {0: [], 1: []}
                    for (kget, vget, nk, ab, kres) in ktiles:
                        banks = [next_bank(), next_bank()]
                        for hf in range(2):
                            sc.op("pe", L("matmul", banks[hf].t[0:nk, 0:W].rearrange("p (r q) -> p r q", r=4),
                                          lhsT=kget(hf, kb), rhs=qT[hf * 64:(hf + 1) * 64, kb * 4:(kb + 1) * 4, q0:q0 + nq], start=True, stop=True),
                                  reads=[res("qT"), kres[0]], excl=[banks[hf]])
                        for hf in range(2):
                            sc.op("act", L("activation", out=ptraw[0:nk, hf, 0:W], in_=banks[hf].t[0:nk, 0:W], func=AF.Exp, scale=0.125),
                                  writes=[res("ptraw%d" % hf)], excl=[banks[hf]])
                        for hf in range(2):
                            g = 2 * kb + hf
                            pi = pt_ctr[0] % 4
                            pt_ctr[0] += 1
                            sc.op("dve", L("tensor_tensor", out=pt[0:nk, pi, 0:W].rearrange("p (r q) -> p r q", r=4),
                                           in0=ptraw[0:nk, hf, 0:W].rearrange("p (r q) -> p r q", r=4),
                                           in1=E[0:nk, ab, g * 4:(g + 1) * 4, 0:nq], op=ALU.mult),
                                  reads=[res("ptraw%d" % hf), res("E")], writes=[res("pt%d" % pi)])
                            ptl[hf].append((pi, nk, vget, kres))
                    return [(0, 2 * kb, ptl[0]), (1, 2 * kb + 1, ptl[1])]

                def pv(allpts, bo, bd):
                    nkt = len(allpts[0][2])
                    for idx in range(nkt):
                        first, lastk = (idx == 0), (idx == nkt - 1)
                        for bank, isden in ((bo, False), (bd, True)):
                            for (hf, g, pts) in allpts:
                                pi, nk, vget, kres = pts[idx]
                                lhs = ones_bf[0:nk, :] if isden else vget(g)
                                sc.op("pe", L("matmul", bank.t[hf * 64:(hf + 1) * 64, 0:W], lhsT=lhs, rhs=pt[0:nk, pi, 0:W], start=first, stop=lastk),
                                      reads=[res("pt%d" % pi), res("ones")] + kres, excl=[bank], signal=(lastk and hf == 1))

                def norm(kb, bo, bd):
                    stg = dsb if kb == 0 else o1
                    sres = "dsb" if kb == 0 else "o1"
                    for r in range(4):
                        j = kb * 4 + r
                        sc.op("act", L("activation", out=stg[:, r * nq:(r + 1) * nq], in_=bd.t[:, r * nq:(r + 1) * nq], func=AF.Ln, bias=esink[:, j:j + 1]),
                              reads=[res("esink")], writes=[res(sres)], excl=[bd])
                    sc.op("act", L("activation", out=stg[:, 0:W], in_=stg[:, 0:W], func=AF.Exp, scale=-1.0), reads=[res(sres)], writes=[res(sres)])
                    sc.op("dve", L("tensor_tensor", out=stg[:, 0:W], in0=bo.t[:, 0:W], in1=stg[:, 0:W], op=ALU.mult), reads=[res(sres)], writes=[res(sres)], excl=[bo])
                    sc.op("pool", L("tensor_tensor", out=a_in[:, kb * 4:(kb + 1) * 4, q0:q0 + nq], in0=stg[:, 0:W].rearrange("p (r q) -> p r q", r=4),
                                    in1=g_ag[:, kb * 4:(kb + 1) * 4, q0:q0 + nq], op=ALU.mult),
                          reads=[res(sres), res("g_ag")], writes=[res("a_in")])

                pk0 = scores_kb(0)
                yield
                pv(pk0, BO, BD)
                yield
                pk1 = scores_kb(1)
                yield
                norm(0, BO, BD)
                bo1, bd1 = next_bank(), next_bank()
                pv(pk1, bo1, bd1)
                norm(1, bo1, bd1)
                yield

        def pool_pre(t):
            n = 128 if t < NT else 32
            for g in range(4):
                w = 2 << g
                b0 = 2 * g
                mres = res("mixed%d" % g)
                if t < NT:
                    LL = 16 + n
                    units = [(lambda lo, hi, b0=b0: pu_sb[:, b0:b0 + 2, lo:hi],
                              lambda lo, hi: sA[:, :, lo:hi],
                              lambda lo, hi: sB[:, :, lo:hi],
                              lambda lo, hi, b0=b0: mixed[:, b0:b0 + 2, lo:hi])]
                    ures = res("pu")
                else:
                    LL = 32
                    units = []
                    for bb in range(2):
                        units.append((lambda lo, hi, bb=bb, b0=b0: pus[:, b0 + bb, :, lo:hi],
                                      lambda lo, hi, bb=bb: sA[:, bb, 0:64].rearrange("p (s c) -> p s c", s=2)[:, :, lo:hi],
                                      lambda lo, hi, bb=bb: sB[:, bb, 0:64].rearrange("p (s c) -> p s c", s=2)[:, :, lo:hi],
                                      lambda lo, hi, bb=bb, b0=b0: mixed[:, b0 + bb, 0:32].rearrange("p (s c) -> p s c", s=2)[:, :, lo:hi]))
                    ures = res("pus")
                for (U, A, B, MX) in units:
                    sc.op("pool", L("tensor_tensor", out=A(2, LL), in0=U(2, LL), in1=U(1, LL - 1), op=ALU.add), reads=[ures], writes=[res("sA")], extra=[c_all, WEV.get("c3")])
                    fin, fres = A, res("sA")
                    if w >= 4:
                        sc.op("pool", L("tensor_tensor", out=B(4, LL), in0=A(4, LL), in1=A(2, LL - 2), op=ALU.add), reads=[res("sA")], writes=[res("sB")])
                        fin, fres = B, res("sB")
                    if w >= 8:
                        sc.op("pool", L("tensor_tensor", out=A(8, LL), in0=B(8, LL), in1=B(4, LL - 4), op=ALU.add), reads=[res("sB")], writes=[res("sA")])
                        fin, fres = A, res("sA")
                    if w >= 16:
                        sc.op("pool", L("tensor_tensor", out=B(16, LL), in0=A(16, LL), in1=A(8, LL - 8), op=ALU.add), reads=[res("sA")], writes=[res("sB")])
                        fin, fres = B, res("sB")
                    sc.op("dve", L("scalar_tensor_tensor", out=MX(0, LL - 16), in0=fin(16, LL), scalar=1.0 / w, in1=U(16, LL), op0=ALU.mult, op1=ALU.subtract),
                          reads=[fres, ures], writes=[mres])
                    if t == 0:
                        sc.op("pool", L("tensor_tensor", out=fin(16, 32), in0=fin(16, 32), in1=invc[:, b0:b0 + 2, :], op=ALU.mult), reads=[fres, mres, res("mT")], writes=[fres], extra=[c_all])
                        sc.op("dve", L("tensor_tensor", out=MX(0, 16), in0=fin(16, 32), in1=U(16, 32), op=ALU.subtract), reads=[fres, ures], writes=[mres])
            if t < NT - 1:
                sc.op("pool", L("tensor_copy", out=pu_sb[:, :, 1:16], in_=pu_sb[:, :, n + 1:n + 16]), reads=[res("pu")], writes=[res("pu")])

        def pool_out(t):
            if t == NT - 1:
                jobs = [(pp, lambda blk: pu_sb[:, blk, 129:144], res("pu"))]
            else:
                jobs = [(spo[bq], (lambda blk, bq=bq: pus[:, blk, bq, 17:32]), res("pus")) for bq in range(2)]
            for (dst, getter, rr) in jobs:
                for half, (stg, sres) in enumerate(((o1, "o1"), (dsb, "dsb"))):
                    b = next_bank()
                    for i in range(4):
                        blk = 4 * half + i
                        sc.op("pe", L("transpose", out=b.t[0:15, i * 128:(i + 1) * 128], in_=getter(blk), identity=identf[:, :]),
                              reads=[rr], excl=[b], extra=[c_all], signal=(i == 3))
                    sc.op("act", L("activation", out=stg[0:15, :], in_=b.t[0:15, :], func=AF.Copy), writes=[res(sres)], excl=[b])
                    sc.dma("sp", L("dma_start", out=dst[:, half * 512:(half + 1) * 512], in_=stg[0:15, :]), "o_pool_" + sres, reads=[res(sres)], is_output=True)
                    yield

        def pool_post(t):
            n = 128 if t < NT else 32
            for gp in range(2):
                bpo = next_bank()
                for gg in range(2):
                    g = 2 * gp + gg
                    b0 = 2 * g
                    for ob in range(2):
                        for kc in range(2):
                            sc.op("pe", L("matmul", bpo.t[:, (2 * gg + ob) * 128:(2 * gg + ob) * 128 + n], lhsT=pw_sb[:, g, kc, ob * 128:(ob + 1) * 128], rhs=mixed[:, b0 + kc, 0:n], start=(kc == 0), stop=(kc == 1)),
                                  reads=[res("mixed%d" % g)], excl=[bpo], extra=[WEV["w_pw"]], signal=(gg == 1 and ob == 1 and kc == 1))
                for i in range(4):
                    blk = 4 * gp + i
                    sc.op("dve", L("scalar_tensor_tensor", out=p_in[:, blk, 0:n], in0=bpo.t[:, i * 128:i * 128 + n], scalar=psc8[:, blk:blk + 1], in1=g_pg[:, blk, 0:n], op0=ALU.mult, op1=ALU.mult),
                          reads=[res("g_pg")], writes=[res("p_in")], excl=[bpo], extra=[c_all])
                yield

        def gate(t):
            n = 128 if t < NT else 32
            for jp in range(4):
                b = next_bank()
                for which, (wsb_, src_, rname, wk) in enumerate(((wa_sb, a_in, "a_in", "w_wa"), (wp_sb, p_in, "p_in", "w_wp"))):
                    for jj in range(2):
                        jb = 2 * jp + jj
                        sl_ = which * 2 + jj
                        for kc in range(8):
                            sc.op("pe", L("matmul", b.t[:, sl_ * 128:sl_ * 128 + n], lhsT=wsb_[:, kc, jb * 128:(jb + 1) * 128], rhs=src_[:, kc, 0:n], start=(kc == 0), stop=(kc == 7)),
                                  reads=[res(rname)], excl=[b], extra=[WEV[wk]], signal=(which == 1 and jj == 1 and kc == 7))
                for jj in range(2):
                    jb = 2 * jp + jj
                    tj = t_ctr[0] % 2
                    t_ctr[0] += 1
                    sc.op("dve", L("scalar_tensor_tensor", out=t1[:, tj, 0:n], in0=th_ma[:, jb, 0:n], scalar=1.0, in1=b.t[:, jj * 128:jj * 128 + n], op0=ALU.add, op1=ALU.mult),
                          reads=[res("th_ma")], writes=[res("t1_%d" % tj)], excl=[b])
                    sc.op("dve", L("scalar_tensor_tensor", out=t2[:, tj, 0:n], in0=th_mp[:, jb, 0:n], scalar=1.0, in1=b.t[:, (2 + jj) * 128:(2 + jj) * 128 + n], op0=ALU.add, op1=ALU.mult),
                          reads=[res("th_mp")], writes=[res("t2_%d" % tj)], excl=[b])
                    sc.op("pool", L("tensor_tensor", out=mT[:, jb, 0:n], in0=t1[:, tj, 0:n], in1=t2[:, tj, 0:n], op=ALU.add),
                          reads=[res("t1_%d" % tj), res("t2_%d" % tj)], writes=[res("mT")])
                yield

        def final(t):
            n = 128 if t < NT else 32
            sl = t % 2
            for nb_, bk in enumerate((BO, BD)):
                for kc in range(8):
                    sc.op("pe", L("matmul", bk.t[0:n, :], lhsT=mT[:, kc, 0:n], rhs=wo_sb[:, kc, nb_ * 512:(nb_ + 1) * 512], start=(kc == 0), stop=(kc == 7)),
                          reads=[res("mT")], excl=[bk], extra=[WEV["w_wo"]], signal=(kc == 7))
                sc.op("dve", L("scalar_tensor_tensor", out=xbuf[0:n, sl, nb_ * 512:(nb_ + 1) * 512], in0=bk.t[0:n, :], scalar=0.5, in1=xbuf[0:n, sl, nb_ * 512:(nb_ + 1) * 512], op0=ALU.mult, op1=ALU.add),
                      reads=[res("xb%d" % sl)], writes=[res("xb%d" % sl)], excl=[bk])
                yield
            dst_ = yp[t * 128:(t + 1) * 128, :] if t < NT else ys
            sc.dma("sp", L("dma_start", out=dst_, in_=xbuf[0:n, sl, :]), "y%d" % sl, reads=[res("xb%d" % sl)], is_output=True)

        def once(fn, *a):
            fn(*a)
            yield

        def interleave(a, b, na=1, nb=1, until_a=False):
            da = db = False
            while not ((da and db) or (until_a and da)):
                for _ in range(na):
                    if not da:
                        try:
                            next(a)
                        except StopIteration:
                            da = True
                for _ in range(nb):
                    if not db:
                        try:
                            next(b)
                        except StopIteration:
                            db = True

        def sample_state():
            sc.dma("pool", L("dma_start", out=stk_tm[:, :, :], in_=sk.rearrange("b t c -> t b c")), "st_k", writes=[res("mixed0"), res("mixed1")])
            sc.dma("pool", L("dma_start", out=vst[:, :, :], in_=sv.rearrange("b t c -> t b c")), "st_v", writes=[res("pu")])
            for bq in range(2):
                for kb in range(2):
                    sc.op("pe", L("transpose", out=psT[:, bq * 2 + kb, :], in_=stk_tm[:, bq, kb * 128:(kb + 1) * 128], identity=ident_bf[:, :]),
                          extra=[WEV["c2"]], reads=[res("mixed0"), res("mixed1")], excl=[BT], signal=(bq == 1 and kb == 1))
            sc.op("act", L("activation", out=kTst[:, :, :, :].rearrange("p a b c -> p (a b c)"), in_=psT[:, 0:4, :].rearrange("p a c -> p (a c)"), func=AF.Copy),
                  writes=[res("pu")], excl=[BT])

        def front1(t):
            last = (t >= NT - 1)
            if t == NT:
                sample_state()
            if dbg.get("pipeqk", 0):
                xprep_b(t)
                sk_ = qk_a(t, "k", [0, 1])
                yield
                sq0 = qk_a(t, "q", [0, 1, 2, 3])
                yield
                qk_b(t, "k", [0, 1], sk_)
                sq1 = qk_a(t, "q", [4, 5, 6, 7])
                yield
                qk_b(t, "q", [0, 1, 2, 3], sq0)
                v_group(t)
                yield
                qk_b(t, "q", [4, 5, 6, 7], sq1)
            else:
                xprep_b(t)
                yield
                sk_ = qk_a(t, "k", [0, 1])
                v_group(t)
                qk_b(t, "k", [0, 1], sk_)
                yield
                sq0 = qk_a(t, "q", [0, 1, 2, 3])
                for _ in stage_blocks(t, "pu", only=0):
                    pass
                qk_b(t, "q", [0, 1, 2, 3], sq0)
                yield
                sq1 = qk_a(t, "q", [4, 5, 6, 7])
                for _ in stage_blocks(t, "pu", only=1):
                    pass
                qk_b(t, "q", [4, 5, 6, 7], sq1)
                yield
            if last:
                k_out(t)
            if dbg.get("pipeqk", 0):
                yield from stage_blocks(t, "pu")
            pool_pre(t)
            if last:
                yield from pool_out(t)
            yield from stage_blocks(t, "ag")
            yield from stage_blocks(t, "pg")

        def front2(t):
            yield from stage_blocks(t, "ma")
            yield from pool_post(t)
            yield from stage_blocks(t, "mp")

        def back2(t):
            yield from gate(t)
            yield from final(t)

        def late_setup(i):
            if i < 8:
                for bq in range(2):
                    sc.dma("sp", L("dma_start", out=pus[:, i, bq, 1:16], in_=spl[bq, :, i * 128:(i + 1) * 128].rearrange("t p -> p t"), allow_slow_non_contiguous=True), "c3")
            elif i == 8:
                for bq in range(2):
                    sc.dma("sp", L("dma_start", out=sko[bq, 0:112, :], in_=sk[bq, 16:128, :]), "o_st", is_output=True)
                    sc.dma("sp", L("dma_start", out=svo[bq, 0:112, :], in_=sv[bq, 16:128, :]), "o_st", is_output=True)

        tiles = dbg.get("tiles", list(range(NT + 1)))
        if tiles:
            xprep_a(tiles[0])
            load_perm(C_Q, "q")
            load_perm(C_AG, "ag")
            load_perm(C_MA, "ma", perm=False)
            for _ in front1(tiles[0]):
                pass
            bias_finish()
        for ti, t in enumerate(tiles):
            nxt = tiles[ti + 1] if ti + 1 < len(tiles) else None
            if nxt is not None:
                xload(nxt)
            if ti < 9:
                late_setup(ti)
            if "c3" not in WEV and "c3" in sc.dma_keys and (ti >= 8 or nxt is None or nxt == NT):
                WEV["c3"] = sc.group_total("c3")
            f2 = front2(t)
            att = attention(t)
            done = False
            if dbg.get("oldatt"):
                interleave(att, f2, 2, 1, until_a=True)
                done = True
            while not done:
                for step in ("a", "f", "f", "a"):
                    try:
                        next(att if step == "a" else f2)
                    except StopIteration:
                        if step == "a":
                            done = True
                            break
            if nxt is not None:
                xprep_a(nxt)
            for _ in f2:
                pass
            if nxt is not None:
                b2 = back2(t)
                next(b2)
                interleave(b2, front1(nxt), 1, 1)
            else:
                for _ in back2(t):
                    pass

        if dbg.get("dump"):
            for nm, buf, shape, dt_ in (("qT", qT, [128, 8 * 128], BF16), ("kTr", kTr, [128, 4 * 128], BF16), ("vr", vr, [128, 512], BF16),
                                        ("g_ag", g_ag, [128, 1024], BF16), ("g_pg", g_pg, [128, 1024], BF16), ("a_in", a_in, [128, 1024], BF16),
                                        ("p_in", p_in, [128, 1024], BF16), ("mT", mT, [128, 1024], BF16), ("th_ma", th_ma, [128, 1024], BF16),
                                        ("th_mp", th_mp, [128, 1024], BF16), ("pu_sb", pu_sb, [128, 8 * 144], F32), ("hT", hT, [128, 1024], BF16),
                                        ("E", E, [128, 2 * 16 * 128], BF16)):
                dtn = nc.dram_tensor("dbg_" + nm, shape, dt_, kind="ExternalOutput").ap()
                flat = buf
                nd = len(buf.shape)
                if nd == 3:
                    flat = buf[:, :, :].rearrange("p a b -> p (a b)")
                elif nd == 4:
                    flat = buf[:, :, :, :].rearrange("p a b c -> p (a b c)")
                sc.dma("sp", L("dma_start", out=dtn, in_=flat), "dbg", reads=[res(k) for k in list(R.keys())])
        if dbg.get("waitw"):
            for k in dbg["waitw"]:
                sc.op("act", L("activation", out=st[:, 0, 3:4], in_=st[:, 0, 3:4], func=AF.Copy), extra=[WEV[k]])
        sc.emit(block)
    return nc


def _bucket_onehot():
    rel = (127 - np.arange(384)).astype(np.int32)
    nb = 16
    ret = np.where(rel > 0, nb, 0)
    n = np.abs(rel)
    max_exact = nb // 2
    ratio = np.maximum(n, 1).astype(np.float32) / np.float32(max_exact)
    large = max_exact + (np.log(ratio).astype(np.float32) / np.float32(math.log(128 / max_exact))
                         * np.float32(nb - max_exact)).astype(np.int32)
    large = np.minimum(large, nb - 1)
    bucket = ret + np.where(n < max_exact, n, large)
    G = np.zeros((32, 384), np.float32)
    G[bucket, np.arange(384)] = 1.0
    return G


_CACHE = {}


def kernel(x_prompt, x_sample, state_attn_k, state_attn_v, state_pool, norm_gain, w_in,
           q_norm_gain, k_norm_gain, attn_sinks, rel_bias, pool_w, pool_scale,
           w_attn_br, w_pool_br, w_out):
    f = lambda a: np.ascontiguousarray(np.asarray(a, dtype=np.float32))
    if "nc" not in _CACHE:
        _CACHE["nc"] = build_program()
        _CACHE["G"] = _bucket_onehot()
    nc = _CACHE["nc"]
    bd = np.zeros((128, 128), np.float32)
    bd[0:64, 0:64] = 1.0 / 64.0
    bd[64:128, 64:128] = 1.0 / 64.0
    inv = np.zeros((128, 8, 16), np.float32)
    for blk in range(8):
        w = 2 << (blk // 2)
        inv[:, blk, :] = 1.0 / np.minimum(w, np.arange(16) + 1)
    common = {
        "ng": f(norm_gain), "win": f(w_in[0]), "qg": f(q_norm_gain), "kg": f(k_norm_gain),
        "sinks": f(attn_sinks), "rb": f(rel_bias), "pw": f(pool_w[0]), "psc": f(pool_scale),
        "wa": f(w_attn_br[0]), "wp": f(w_pool_br[0]), "wo": f(w_out[0]),
        "cG": _CACHE["G"], "cI": np.eye(128, dtype=np.float32), "cBD": bd, "cINV": inv.reshape(128, 128),
    }
    x_prompt = f(x_prompt)
    x_sample = f(x_sample)
    sk_ = f(state_attn_k)[0].reshape(16, 128, 256)
    sv_ = f(state_attn_v)[0].reshape(16, 128, 256)
    spl_ = f(state_pool)[0]
    in_maps = []
    for c in range(NCORES):
        m = dict(common)
        m["xp"] = x_prompt[c]
        m["xs"] = np.ascontiguousarray(x_sample[2 * c:2 * c + 2].reshape(32, D))
        m["sk"] = np.ascontiguousarray(sk_[2 * c:2 * c + 2])
        m["sv"] = np.ascontiguousarray(sv_[2 * c:2 * c + 2])
        m["spl"] = np.ascontiguousarray(spl_[2 * c:2 * c + 2])
        in_maps.append(m)
    res = run_bass_kernel_spmd(nc, in_maps, core_ids=list(range(NCORES)))
    r = res.results
    y_p = np.stack([r[c]["yp"] for c in range(NCORES)], 0)
    y_s = np.concatenate([r[c]["ys"].reshape(2, 16, D) for c in range(NCORES)], 0)
    p_k = np.stack([r[c]["pk"].reshape(128, 4, 64) for c in range(NCORES)], 0)[None]
    p_v = np.stack([r[c]["pv"].reshape(128, 4, 64) for c in range(NCORES)], 0)[None]
    p_p = np.stack([r[c]["pp"] for c in range(NCORES)], 0)[None]
    s_k = np.concatenate([r[c]["sko"].reshape(2, 128, 4, 64) for c in range(NCORES)], 0)[None]
    s_v = np.concatenate([r[c]["svo"].reshape(2, 128, 4, 64) for c in range(NCORES)], 0)[None]
    s_p = np.concatenate([r[c]["spo"] for c in range(NCORES)], 0)[None]
    return (y_p.astype(np.float32), y_s.astype(np.float32), p_k.astype(np.float32), p_v.astype(np.float32),
            p_p.astype(np.float32), s_k.astype(np.float32), s_v.astype(np.float32), s_p.astype(np.float32))
```
